# Optimizing a Trainium2 kernel written in Bass

```python
import jax, jax.numpy as jnp
from jax import lax
import numpy as np

D_MODEL = 1024
BATCH = 16
SEQ = 4096
DEPTH = 1
DEC_BATCH = 2
DEC_SEQ = 8192
PAST_LEN = 128

GRID_W = 64
D_MIX = D_MODEL
ATT_HEADS = 8
HEAD_DIM = 64
D_ATT = ATT_HEADS * HEAD_DIM
D_LRU = D_MIX - D_ATT
LRU_HEADS = 8
LRU_BLOCK = D_LRU // LRU_HEADS
NA_ROWS = 8
NA_COLS = 16
LRU_CONV = 4
LRU_C = 8.0
FFN_CONV = 3
D_FF = 2816
D_IN = 3 * D_ATT + 2 * D_LRU
N_MOD = 6
NORM_EPS = 1e-6

kernel_name = 'hybrid_natten_rglru_encoder'


def rms_norm(x, g):
    xf = x.astype(jnp.float32)
    y = xf * lax.rsqrt(jnp.mean(xf * xf, axis=-1, keepdims=True) + NORM_EPS)
    return (y * g.astype(jnp.float32)).astype(x.dtype)


def depthwise_conv(x, w, b):
    k = w.shape[0]
    lo = (k - 1) // 2
    y = lax.conv_general_dilated(
        x, w.astype(x.dtype)[:, None, :], window_strides=(1,), padding=[(lo, k - 1 - lo)],
        dimension_numbers=('NWC', 'WIO', 'NWC'), feature_group_count=x.shape[-1])
    return y + b.astype(x.dtype)


def neighbourhood_attention(q, k, v, rpb):
    bsz, length, heads, dh = q.shape
    rows = length // GRID_W
    wr = min(NA_ROWS, rows)
    qg = q.reshape(bsz, rows, GRID_W, heads, dh)
    kg = k.reshape(bsz, rows, GRID_W, heads, dh)
    vg = v.reshape(bsz, rows, GRID_W, heads, dh)
    cols = jnp.arange(GRID_W)
    col_start = jnp.clip(cols - NA_COLS // 2, 0, GRID_W - NA_COLS)
    col_idx = col_start[:, None] + jnp.arange(NA_COLS)[None, :]
    dc = col_idx - cols[:, None] + (NA_COLS - 1)
    scale = HEAD_DIM ** -0.5

    def row_block(r):
        rs = jnp.clip(r - wr // 2, 0, rows - wr)
        qr = lax.dynamic_index_in_dim(qg, r, axis=1, keepdims=False)
        kb = lax.dynamic_slice_in_dim(kg, rs, wr, axis=1)
        vb = lax.dynamic_slice_in_dim(vg, rs, wr, axis=1)
        kw = kb[:, :, col_idx]
        vw = vb[:, :, col_idx]
        dr = rs + jnp.arange(wr) - r + (NA_ROWS - 1)
        bias = rpb[:, dr[None, :, None], dc[:, None, :]]
        s = jnp.einsum('bchd,brcjhd->bhcrj', qr, kw).astype(jnp.float32) * scale
        s = s + bias.astype(jnp.float32)[None]
        p = jax.nn.softmax(s, axis=(-2, -1))
        return jnp.einsum('bhcrj,brcjhd->bchd', p.astype(v.dtype), vw)

    out = lax.map(row_block, jnp.arange(rows))
    return jnp.transpose(out, (1, 0, 2, 3, 4)).reshape(bsz, length, heads * dh)


def _lin_combine(e1, e2):
    a1, b1 = e1
    a2, b2 = e2
    return a1 * a2, a2 * b1 + b2


def rg_lru_direction(x, w_r, b_r, w_i, b_i, lam, reverse):
    bsz, t, _ = x.shape
    xb = x.reshape(bsz, t, LRU_HEADS, LRU_BLOCK)
    r = jax.nn.sigmoid((jnp.einsum('bthd,hde->bthe', xb, w_r).reshape(bsz, t, D_LRU) + b_r).astype(jnp.float32))
    i = jax.nn.sigmoid((jnp.einsum('bthd,hde->bthe', xb, w_i).reshape(bsz, t, D_LRU) + b_i).astype(jnp.float32))
    log_a = -LRU_C * r * jax.nn.softplus(-lam.astype(jnp.float32))
    a = jnp.exp(log_a)
    u = jnp.sqrt(-jnp.expm1(2.0 * log_a)) * (i * x.astype(jnp.float32))
    _, h = lax.associative_scan(_lin_combine, (a, u), reverse=reverse, axis=1)
    return h


def encoder_layer(x, c, ln1_g, ln2_g, w_ada, b_ada, w_in, q_norm_g, k_norm_g, rpb,
                  lru_conv_w, lru_conv_b, w_r_f, b_r_f, w_i_f, b_i_f, lam_f,
                  w_r_b, b_r_b, w_i_b, b_i_b, lam_b, w_out, w_up, ffn_conv_w, ffn_conv_b, w_down):
    bsz, t, _ = x.shape
    mod = (jax.nn.silu(c) @ w_ada + b_ada)[:, None, :]
    sh1, sc1, g1, sh2, sc2, g2 = jnp.split(mod, N_MOD, axis=-1)
    n = rms_norm(x, ln1_g) * (1 + sc1) + sh1
    z = n @ w_in
    q, k, v, xl, gl = jnp.split(z, [D_ATT, 2 * D_ATT, 3 * D_ATT, 3 * D_ATT + D_LRU], axis=-1)
    q = rms_norm(q.reshape(bsz, t, ATT_HEADS, HEAD_DIM), q_norm_g)
    k = rms_norm(k.reshape(bsz, t, ATT_HEADS, HEAD_DIM), k_norm_g)
    v = v.reshape(bsz, t, ATT_HEADS, HEAD_DIM)
    att = neighbourhood_attention(q, k, v, rpb)
    xl = depthwise_conv(xl, lru_conv_w, lru_conv_b)
    h = (rg_lru_direction(xl, w_r_f, b_r_f, w_i_f, b_i_f, lam_f, False)
         + rg_lru_direction(xl, w_r_b, b_r_b, w_i_b, b_i_b, lam_b, True))
    lru = (h * jax.nn.gelu(gl.astype(jnp.float32))).astype(x.dtype)
    mix = jnp.concatenate([att, lru], axis=-1) @ w_out
    x = x + g1 * mix
    n2 = rms_norm(x, ln2_g) * (1 + sc2) + sh2
    up = depthwise_conv(n2 @ w_up, ffn_conv_w, ffn_conv_b)
    ug, uv = jnp.split(up, 2, axis=-1)
    x = x + g2 * ((jax.nn.gelu(ug) * uv) @ w_down)
    return x


def trunk(x, c, params):
    for layer in range(DEPTH):
        x = encoder_layer(x, c, *[p[layer] for p in params])
    return x


def setup_inputs(seed: int = 0) -> dict:
    key = jax.random.key(seed)
    ks = jax.random.split(key, 32)

    def nrm(k, shape, scale):
        return jax.random.normal(k, shape, jnp.float32) * scale

    lam_u = jax.random.uniform(ks[30], (2, DEPTH, D_LRU), jnp.float32, 0.9, 0.999)
    a0 = lam_u ** (1.0 / LRU_C)
    lam0 = jnp.log(a0) - jnp.log1p(-a0)
    gate_s = LRU_BLOCK ** -0.5
    return {
        'x_prompt': nrm(ks[0], (BATCH, SEQ, D_MODEL), 1.0),
        'x_sample': nrm(ks[1], (DEC_BATCH, DEC_SEQ, D_MODEL), 1.0),
        'c_prompt': nrm(ks[2], (BATCH, D_MODEL), 1.0),
        'c_sample': nrm(ks[3], (DEC_BATCH, D_MODEL), 1.0),
        'ln1_g': 1.0 + nrm(ks[4], (DEPTH, D_MODEL), 0.02),
        'ln2_g': 1.0 + nrm(ks[5], (DEPTH, D_MODEL), 0.02),
        'w_ada': nrm(ks[6], (DEPTH, D_MODEL, N_MOD * D_MODEL), 0.5 * D_MODEL ** -0.5),
        'b_ada': nrm(ks[7], (DEPTH, N_MOD * D_MODEL), 0.02),
        'w_in': nrm(ks[8], (DEPTH, D_MODEL, D_IN), D_MODEL ** -0.5),
        'q_norm_g': 1.0 + nrm(ks[9], (DEPTH, HEAD_DIM), 0.02),
        'k_norm_g': 1.0 + nrm(ks[10], (DEPTH, HEAD_DIM), 0.02),
        'rpb': nrm(ks[11], (DEPTH, ATT_HEADS, 2 * NA_ROWS - 1, 2 * NA_COLS - 1), 0.1),
        'lru_conv_w': nrm(ks[12], (DEPTH, LRU_CONV, D_LRU), LRU_CONV ** -0.5),
        'lru_conv_b': nrm(ks[13], (DEPTH, D_LRU), 0.02),
        'w_r_f': nrm(ks[14], (DEPTH, LRU_HEADS, LRU_BLOCK, LRU_BLOCK), gate_s),
        'b_r_f': nrm(ks[15], (DEPTH, D_LRU), 0.02),
        'w_i_f': nrm(ks[16], (DEPTH, LRU_HEADS, LRU_BLOCK, LRU_BLOCK), gate_s),
        'b_i_f': nrm(ks[17], (DEPTH, D_LRU), 0.02),
        'lam_f': lam0[0],
        'w_r_b': nrm(ks[18], (DEPTH, LRU_HEADS, LRU_BLOCK, LRU_BLOCK), gate_s),
        'b_r_b': nrm(ks[19], (DEPTH, D_LRU), 0.02),
        'w_i_b': nrm(ks[20], (DEPTH, LRU_HEADS, LRU_BLOCK, LRU_BLOCK), gate_s),
        'b_i_b': nrm(ks[21], (DEPTH, D_LRU), 0.02),
        'lam_b': lam0[1],
        'w_out': nrm(ks[22], (DEPTH, D_MIX, D_MODEL), D_MIX ** -0.5),
        'w_up': nrm(ks[23], (DEPTH, D_MODEL, 2 * D_FF), D_MODEL ** -0.5),
        'ffn_conv_w': nrm(ks[24], (DEPTH, FFN_CONV, 2 * D_FF), FFN_CONV ** -0.5),
        'ffn_conv_b': nrm(ks[25], (DEPTH, 2 * D_FF), 0.02),
        'w_down': nrm(ks[26], (DEPTH, D_FF, D_MODEL), D_FF ** -0.5),
    }


def reference(x_prompt, x_sample, c_prompt, c_sample, ln1_g, ln2_g, w_ada, b_ada, w_in,
              q_norm_g, k_norm_g, rpb, lru_conv_w, lru_conv_b, w_r_f, b_r_f, w_i_f, b_i_f, lam_f,
              w_r_b, b_r_b, w_i_b, b_i_b, lam_b, w_out, w_up, ffn_conv_w, ffn_conv_b, w_down):
    params = (ln1_g, ln2_g, w_ada, b_ada, w_in, q_norm_g, k_norm_g, rpb,
              lru_conv_w, lru_conv_b, w_r_f, b_r_f, w_i_f, b_i_f, lam_f,
              w_r_b, b_r_b, w_i_b, b_i_b, lam_b, w_out, w_up, ffn_conv_w, ffn_conv_b, w_down)
    y_prompt = trunk(x_prompt, c_prompt, params)
    y_sample = trunk(x_sample, c_sample, params)
    return (y_prompt, y_sample)
```

```python
import os
import numpy as np
from contextlib import ExitStack
import concourse.bass as bass
import concourse.mybir as mybir
from concourse.bass_utils import run_bass_kernel_spmd

F32 = mybir.dt.float32
BF16 = mybir.dt.bfloat16
AF = mybir.ActivationFunctionType
ALU = mybir.AluOpType

ENGS = ("sync", "scalar", "vector", "gpsimd", "tensor")
D = 1024
T = 512
NEG = -30000.0


class Buf:
    __slots__ = ("name", "w", "r", "al")

    def __init__(self, name):
        self.name = name
        self.w = []
        self.r = []
        self.al = []


class Prog:
    def __init__(self, nc, stack, n_dsem=48):
        self.nc = nc
        self.stack = stack
        self.ops = {e: [] for e in ENGS}
        self.cnt = {e: 0 for e in ENGS}
        self.waited = {e: {} for e in ENGS}
        self.esem = {e: stack.enter_context(nc.semaphore("es_" + e)) for e in ENGS}
        self.dsems = [[stack.enter_context(nc.semaphore("ds%d" % i)), 0] for i in range(n_dsem)]
        self.next_dsem = 0
        self.all_bufs = []

    def buf(self, name, dma=False):
        b = Buf(name)
        self.all_bufs.append(b)
        return b

    def new_dsem(self):
        i = self.next_dsem
        self.next_dsem += 1
        assert i < len(self.dsems), "out of dma sems"
        return i

    def reset_dsem_pool(self):
        self.next_dsem = 0

    def _sem_handle(self, key):
        return self.esem[key] if isinstance(key, str) else self.dsems[key][0]

    def _collect(self, eng, reads, writes):
        need = {}
        for b0 in reads:
            for b in [b0] + b0.al:
                for (k, v) in b.w:
                    need[k] = max(need.get(k, 0), v)
        for b0 in writes:
            for b in [b0] + b0.al:
                for (k, v) in b.w:
                    need[k] = max(need.get(k, 0), v)
                for (k, v) in b.r:
                    need[k] = max(need.get(k, 0), v)
        waits = []
        wd = self.waited[eng]
        for k, v in need.items():
            if not isinstance(k, str):
                v = max(v, self.dsems[k][1])
            if wd.get(k, 0) < v:
                wd[k] = v
                waits.append((k, v))
        return waits

    def op(self, eng, fn, reads=(), writes=(), dsem=None):
        waits = self._collect(eng, reads, writes)
        if eng == "tensor":
            waits = [(k, v) for (k, v) in waits if k != "tensor"]
        if dsem is None:
            self.cnt[eng] += 1
            tok = (eng, self.cnt[eng])
            inc = ("e", eng)
        else:
            self.dsems[dsem][1] += 16
            tok = (dsem, self.dsems[dsem][1])
            inc = ("d", dsem)
        self.ops[eng].append((waits, fn, inc))
        for b in writes:
            b.w = [tok]
            b.r = []
        for b in reads:
            if b not in writes:
                b.r.append(tok)
                if len(b.r) > 64:
                    b.r = b.r[-64:]
        return tok

    def barrier(self):
        final = {}
        for e in ENGS:
            if self.cnt[e]:
                final[e] = self.cnt[e]
        for i, (h, c) in enumerate(self.dsems):
            if c:
                final[i] = c
        for e in ENGS:
            waits = []
            for k, v in final.items():
                if self.waited[e].get(k, 0) < v:
                    self.waited[e][k] = v
                    waits.append((k, v))
            if waits:
                self.ops[e].append((waits, None, None))
        for b in self.all_bufs:
            b.w = []
            b.r = []

    def emit(self):
        nc = self.nc
        with nc.Block() as block:
            def run(engname):
                def body(e):
                    for (waits, fn, inc) in self.ops[engname]:
                        for (k, v) in waits:
                            e.wait_ge(self._sem_handle(k), v)
                        if fn is None:
                            continue
                        ins = fn(e)
                        if inc[0] == "e":
                            ins.then_inc(self.esem[inc[1]], 1)
                        else:
                            ins.then_inc(self.dsems[inc[1]][0], 16)
                return body
            block.sync(run("sync"))
            block.scalar(run("scalar"))
            block.vector(run("vector"))
            block.gpsimd(run("gpsimd"))
            block.tensor(run("tensor"))
        for e in ENGS:
            self.ops[e] = []


class Tile:
    def __init__(self, P, t, name, dma=False):
        self.t = t
        self.b = P.buf(name)
        self.ds = P.new_dsem() if dma else None

    def __getitem__(self, idx):
        return self.t[idx]


def build_program(seq_spec, debug=False, phases=(0, 1, 2, 3, 4)):
    seq_spec = [(int(L), bool(m)) for (L, m) in seq_spec]
    NS = sum(2 if m else 1 for (L, m) in seq_spec)
    NTOK = sum(L for (L, m) in seq_spec)
    nc = bass.Bass("TRN2", target_bir_lowering=False)

    def din(name, shape, dt=F32):
        return nc.dram_tensor(name, list(shape), dt, kind="ExternalInput").ap()

    skind = "ExternalOutput" if debug else "Internal"

    def dscr(name, shape, dt=F32):
        return nc.dram_tensor(name, list(shape), dt, kind=skind).ap()

    xs = din("xs", [NTOK, D])
    cT = din("cT", [128, 8, NS])
    w_ada = din("w_ada", [D, 6 * D])
    badaT = din("badaT", [128, 48])
    ln1T = din("ln1T", [128, 8])
    ln2T = din("ln2T", [128, 8])
    w_in = din("w_in", [D, 2560])
    w_out = din("w_out", [D, D])
    w_up = din("w_up", [D, 5632])
    w_down = din("w_down", [2816, D])
    gq_d = din("gq", [128, 1])
    gk_d = din("gk", [128, 1])
    TI_d = din("TI", [128, 8, 11, 64])
    TF_d = din("TF", [128, 8, 15, 64])
    lcw_d = din("lcw", [128, 4, 4])
    lcb_d = din("lcb", [128, 4])
    gate_d = {}
    for dr in ("f", "b"):
        gate_d[dr] = dict(wr=din("wr_" + dr, [128, 4, 128]), wi=din("wi_" + dr, [128, 4, 128]),
                          br=din("br_" + dr, [128, 4]), bi=din("bi_" + dr, [128, 4]),
                          lam=din("lam_" + dr, [128, 4]))
    fcw_d = din("fcw", [128, 44, 3])
    fcb_d = din("fcb", [128, 44])
    ident_d = din("ident", [128, 128])
    flag_d = din("flag", [128, 2])
    ys = nc.dram_tensor("ys", [NTOK, D], F32, kind="ExternalOutput").ap()

    QT = dscr("QT", [512, NTOK], BF16)
    KT = dscr("KT", [512, NTOK], BF16)
    VV = dscr("VV", [NTOK, 512], BF16)
    XL = dscr("XL", [512, NTOK])
    GG = dscr("GG", [512, NTOK])
    HB = dscr("HB", [512, NTOK])
    X1 = dscr("X1", [NTOK, D])
    N2T = dscr("N2T", [D, NTOK], BF16)
    GR = dscr("GR", [2 * NS, D])

    def bcast_rows(ap_row, nparts=128):
        t = ap_row.tensor
        return bass.AP(t, ap_row.offset, [[0, nparts]] + [list(x) for x in ap_row.ap[1:]])

    with ExitStack() as top:
        P = Prog(nc, top)

        def dma(out, in_, reads=(), writes=(), ds=None, eng="sync", slow=False):
            if slow:
                fn = lambda e, o=out, i=in_: e.dma_start(out=o, in_=i, allow_slow_non_contiguous=True)
            else:
                fn = lambda e, o=out, i=in_: e.dma_start(out=o, in_=i)
            return P.op(eng, fn, reads=reads, writes=writes, dsem=ds)

        rr = {"i": 0}

        def any_eng():
            rr["i"] += 1
            return ("vector", "gpsimd")[rr["i"] % 2]

        uid = {"n": 0}

        def sbuf(st, name, shape, dt, dma_=False):
            uid["n"] += 1
            name = "s%d_%s" % (uid["n"], name)
            return Tile(P, st.enter_context(nc.sbuf_tensor(name, list(shape), dt)), name, dma=dma_)

        def psum(st, name, shape, dt):
            uid["n"] += 1
            name = "p%d_%s" % (uid["n"], name)
            return Tile(P, st.enter_context(nc.psum_tensor(name, list(shape), dt)), name)

        identb = sbuf(top, "identb", [128, 128], BF16)
        onesb = sbuf(top, "onesb", [128, 128], BF16)
        blk1 = sbuf(top, "blk1", [128, 128], BF16)
        epsb = sbuf(top, "epsb", [128, 1], F32)
        mod = sbuf(top, "mod", [128, 48, NS], F32)
        G1 = sbuf(top, "G1", [128, 8, NS], F32)
        G2 = sbuf(top, "G2", [128, 8, NS], F32)
        gq8 = sbuf(top, "gq8", [128, 1], F32, True)
        flg = sbuf(top, "flg", [128, 2], F32, True)
        gk1 = sbuf(top, "gk1", [128, 1], F32, True)

        with ExitStack() as st:
            P.reset_dsem_pool()
            P.next_dsem = 4
            idf = sbuf(st, "idf", [128, 128], F32, True)
            cs = sbuf(st, "cs", [128, 8, NS], F32, True)
            scs = sbuf(st, "scs", [128, 8, NS], F32)
            bada = sbuf(st, "bada", [128, 48], F32, True)
            l1 = sbuf(st, "l1", [128, 8], F32, True)
            l2 = sbuf(st, "l2", [128, 8], F32, True)
            wst = [sbuf(st, "wst%d" % i, [128, 8, 512], F32, True) for i in range(2)]
            pmod = psum(st, "pmod", [128, 512], F32)
            tmp = sbuf(st, "tmp0", [128, 8, NS], F32)

            dma(idf[:], ident_d, writes=[idf.b], ds=idf.ds)
            P.op("vector", lambda e: e.tensor_copy(out=identb[:], in_=idf[:]), reads=[idf.b], writes=[identb.b])
            P.op("vector", lambda e: e.memset(onesb[:], 1.0), writes=[onesb.b])
            P.op("vector", lambda e: e.memset(blk1[:], 0.0), writes=[blk1.b])
            P.op("vector", lambda e: e.memset(blk1[0:64, 0:64], 1.0), reads=[blk1.b], writes=[blk1.b])
            P.op("vector", lambda e: e.memset(blk1[64:128, 64:128], 1.0), reads=[blk1.b], writes=[blk1.b])
            P.op("vector", lambda e: e.memset(epsb[:], 1e-6), writes=[epsb.b])
            dma(gq8[:], gq_d, writes=[gq8.b], ds=gq8.ds)
            dma(flg[:], flag_d, writes=[flg.b], ds=flg.ds)
            dma(gk1[:], gk_d, writes=[gk1.b], ds=gk1.ds)
            P.op("vector", lambda e: e.tensor_scalar(out=gq8[:], in0=gq8[:], scalar1=0.125, scalar2=None, op0=ALU.mult),
                 reads=[gq8.b], writes=[gq8.b])
            dma(cs[:], cT, writes=[cs.b], ds=cs.ds)
            dma(bada[:], badaT, writes=[bada.b], ds=bada.ds)
            dma(l1[:], ln1T, writes=[l1.b], ds=l1.ds)
            dma(l2[:], ln2T, writes=[l2.b], ds=l2.ds)
            P.op("scalar", lambda e: e.activation(out=scs[:], in_=cs[:], func=AF.Silu), reads=[cs.b], writes=[scs.b])
            wv = w_ada.rearrange("(k p) n -> p k n", p=128)
            for blk in range(12):
                w = wst[blk % 2]
                dma(w[:], wv[:, :, blk * 512:(blk + 1) * 512], writes=[w.b], ds=w.ds)

                def mm(e, w=w, blk=blk):
                    ins = None
                    for m in range(4):
                        for k in range(8):
                            ins = e.matmul(pmod[:, (blk * 4 + m) * NS:(blk * 4 + m + 1) * NS],
                                           lhsT=w[:, k, m * 128:(m + 1) * 128], rhs=scs[:, k, :],
                                           start=(k == 0), stop=(k == 7))
                    return ins
                P.op("tensor", mm, reads=[w.b, scs.b], writes=[pmod.b])
            pm3 = pmod[:, 0:48 * NS].rearrange("p (m s) -> p m s", s=NS)
            for s in range(NS):
                P.op("vector", lambda e, s=s: e.tensor_tensor(out=mod[:, :, s], in0=pm3[:, :, s], in1=bada[:], op=ALU.add),
                     reads=[pmod.b, bada.b, mod.b], writes=[mod.b])
            for (G, ln, o) in ((G1, l1, 8), (G2, l2, 32)):
                for s in range(NS):
                    P.op("vector", lambda e, s=s, o=o: e.tensor_scalar(out=tmp[:, :, s], in0=mod[:, o:o + 8, s], scalar1=1.0,
                                                                     scalar2=None, op0=ALU.add),
                         reads=[mod.b, tmp.b], writes=[tmp.b])
                    P.op("vector", lambda e, s=s, G=G, ln=ln: e.tensor_tensor(out=G[:, :, s], in0=tmp[:, :, s], in1=ln[:], op=ALU.mult),
                         reads=[tmp.b, ln.b, G.b], writes=[G.b])
            for gi, o in ((0, 16), (1, 40)):
                for s in range(NS):
                    dst = GR[gi * NS + s:gi * NS + s + 1, :].rearrange("o (k p) -> p (o k)", p=128)
                    dma(dst, mod[:, o:o + 8, s], reads=[mod.b], ds=3, slow=True)
            P.barrier()
            P.emit()

        def load_cast(dst, src, K, N, wstg, piece=512):
            sv = src.rearrange("(k p) n -> p k n", p=128)
            i = 0
            for k0 in range(0, K, 8):
                k1 = min(K, k0 + 8)
                for n0 in range(0, N, piece):
                    n1 = min(N, n0 + piece)
                    w = wstg[i % len(wstg)]
                    i += 1
                    dma(w[:, 0:k1 - k0, 0:n1 - n0], sv[:, k0:k1, n0:n1], writes=[w.b], ds=w.ds)
                    eng = ("vector", "gpsimd", "scalar")[i % 3]
                    if eng == "scalar":
                        P.op(eng, lambda e, w=w, k0=k0, k1=k1, n0=n0, n1=n1: e.activation(
                            out=dst[:, k0:k1, n0:n1], in_=w[:, 0:k1 - k0, 0:n1 - n0], func=AF.Copy),
                            reads=[w.b], writes=[])
                    else:
                        P.op(eng, lambda e, w=w, k0=k0, k1=k1, n0=n0, n1=n1: e.tensor_copy(
                            out=dst[:, k0:k1, n0:n1], in_=w[:, 0:k1 - k0, 0:n1 - n0]),
                            reads=[w.b], writes=[])

        def rms_to_featmajor(xt, ss, rs, xh, junk, tp, nT, Gm, shm, s, sh_off=0, ev_engs=("scalar", "vector")):
            for sub in range(4):
                P.op("scalar", lambda e, sub=sub: e.activation(out=junk[:], in_=xt[:, sub, :], func=AF.Square,
                                                              accum_out=ss[:, sub:sub + 1]),
                     reads=[xt.b, ss.b], writes=[junk.b, ss.b])
            P.op("scalar", lambda e: e.activation(out=rs[:], in_=ss[:], func=AF.Sqrt, scale=1.0 / D, bias=epsb[:, 0:1]),
                 reads=[ss.b, epsb.b], writes=[rs.b])
            P.op("vector", lambda e: e.reciprocal(out=rs[:], in_=rs[:]), reads=[rs.b], writes=[rs.b])
            for sub in range(4):
                if sub % 2 == 0:
                    P.op("vector", lambda e, sub=sub: e.tensor_scalar(out=xh[:, sub, :], in0=xt[:, sub, :], scalar1=rs[:, sub:sub + 1],
                                                                     scalar2=None, op0=ALU.mult),
                         reads=[xt.b, rs.b, xh.b], writes=[xh.b])
                else:
                    P.op("scalar", lambda e, sub=sub: e.activation(out=xh[:, sub, :], in_=xt[:, sub, :], func=AF.Copy,
                                                                  scale=rs[:, sub:sub + 1]),
                         reads=[xt.b, rs.b, xh.b], writes=[xh.b])
            for kk in range(4):
                tpb = tp[kk % 2]

                def tr(e, kk=kk, tpb=tpb):
                    ins = None
                    for j in range(2):
                        k = 2 * kk + j
                        for sub in range(4):
                            ins = e.transpose(out=tpb[:, j * 512 + sub * 128: j * 512 + (sub + 1) * 128],
                                              in_=xh[:, sub, k * 128:(k + 1) * 128], identity=identb[:])
                    return ins
                P.op("tensor", tr, reads=[xh.b, identb.b], writes=[tpb.b])
                for j in range(2):
                    k = 2 * kk + j
                    eng = ev_engs[j % len(ev_engs)]
                    if eng == "scalar":
                        P.op("scalar", lambda e, k=k, j=j, tpb=tpb: e.activation(
                            out=nT[:, k, :], in_=tpb[:, j * 512:(j + 1) * 512], func=AF.Identity,
                            scale=Gm[:, k, s:s + 1], bias=shm[:, sh_off + k, s:s + 1]),
                            reads=[tpb.b, Gm.b, mod.b, nT.b], writes=[nT.b])
                    else:
                        P.op(eng, lambda e, k=k, j=j, tpb=tpb: e.tensor_scalar(
                            out=nT[:, k, :], in0=tpb[:, j * 512:(j + 1) * 512], scalar1=Gm[:, k, s:s + 1],
                            scalar2=shm[:, sh_off + k, s:s + 1], op0=ALU.mult, op1=ALU.add),
                            reads=[tpb.b, Gm.b, mod.b, nT.b], writes=[nT.b])

        xs_v = xs.rearrange("(n s p) d -> n p s d", s=4, p=128)
        ys_v = ys.rearrange("(n s p) d -> n p s d", s=4, p=128)
        X1_v = X1.rearrange("(n s p) d -> n p s d", s=4, p=128)
        QT_v = QT.rearrange("(c p) t -> p c t", p=128)
        KT_v = KT.rearrange("(c p) t -> p c t", p=128)
        XL_v = XL.rearrange("(c p) t -> p c t", p=128)
        GG_v = GG.rearrange("(c p) t -> p c t", p=128)
        HB_v = HB.rearrange("(c p) t -> p c t", p=128)
        N2_v = N2T.rearrange("(c p) t -> p c t", p=128)

        tiles = []
        seq_tiles = []
        g_ = 0
        m_ = 0
        for q, (L, has_mid) in enumerate(seq_spec):
            nt = L // T
            lst = []
            for b in range(nt):
                mid = None
                ms = m_
                if has_mid:
                    ms = m_ + (1 if b >= nt // 2 else 0)
                    if b == nt // 2 - 1:
                        mid = "lo"
                    elif b == nt // 2:
                        mid = "hi"
                lst.append((ms, b, g_ + b * T, nt, q, mid))
            tiles += lst
            seq_tiles.append(lst)
            g_ += L
            m_ += 2 if has_mid else 1

        if 1 in phases:
            with ExitStack() as st:
                P.reset_dsem_pool()
                P.next_dsem = 4
                winb = sbuf(st, "winb", [128, 8, 2560], BF16)
                wstg = [sbuf(st, "wstg%d" % i, [128, 8, 512], F32, True) for i in range(2)]
                load_cast(winb, w_in, 8, 2560, wstg)
                P.barrier()
                xt = [sbuf(st, "xt%d" % i, [128, 4, D], F32, True) for i in range(2)]
                junk = sbuf(st, "junk", [128, D], BF16)
                xh2 = [sbuf(st, "xh%d" % i, [128, 4, D], BF16) for i in range(2)]
                nT2 = [sbuf(st, "nT%d" % i, [128, 8, T], BF16) for i in range(2)]
                ss2 = [sbuf(st, "ssb%d" % i, [128, 4], F32) for i in range(2)]
                rs2 = [sbuf(st, "rsb%d" % i, [128, 4], F32) for i in range(2)]
                tpf = [psum(st, "tp%d" % i, [128, 512], F32) for i in range(2)]
                tp = [Tile.__new__(Tile) for _ in range(2)]
                for i in range(2):
                    tp[i].t = tpf[i].t.bitcast(BF16)
                    tp[i].b = tpf[i].b
                    tp[i].ds = None
                mmp = [psum(st, "mm%d" % i, [128, 512], F32) for i in range(4)]
                ssp = [psum(st, "ssp%d" % i, [128, 512], F32) for i in range(2)]
                sq = [sbuf(st, "sq%d" % i, [128, T], BF16) for i in range(2)]
                sd = [sbuf(st, "sd%d" % i, [128, T], F32) for i in range(2)]
                qn = [sbuf(st, "qn%d" % i, [128, T], BF16, True) for i in range(3)]
                of = [sbuf(st, "of%d" % i, [128, T], F32, True) for i in range(3)]
                vb = [sbuf(st, "vb%d" % i, [128, 512], BF16, True) for i in range(2)]
                def load_x(i):
                    (s, b, g0, nt, q_, mid_) = tiles[i]
                    x_ = xt[i % 2]
                    dma(x_[:], xs_v[g0 // T], writes=[x_.b], ds=x_.ds)

                def stage_a(i):
                    (s_, b_, g0_, nt_, q_, mid_) = tiles[i]
                    ss_, rs_ = ss2[i % 2], rs2[i % 2]
                    P.op("vector", lambda e, ss_=ss_: e.memset(ss_[:], 0.0), reads=[ss_.b], writes=[ss_.b])
                    rms_to_featmajor(xt[i % 2], ss_, rs_, xh2[i % 2], junk, tp, nT2[i % 2], G1, mod, s_)

                load_x(0)
                if len(tiles) > 1:
                    load_x(1)
                stage_a(0)
                cnt = {"qk": 0, "of": 0, "mm": 0, "v": 0}
                pend = []
                for i, (s, b, g0, nt, q_, mid_) in enumerate(tiles):
                    if i + 1 < len(tiles):
                        stage_a(i + 1)
                    if i + 2 < len(tiles):
                        load_x(i + 2)
                    nT = nT2[i % 2]
                    for m in list(range(0, 8)) + list(range(12, 20)):
                        pb = mmp[cnt["mm"] % 4]
                        cnt["mm"] += 1

                        def mmf(e, m=m, pb=pb, nT=nT):
                            ins = None
                            for k in range(8):
                                ins = e.matmul(pb[:], lhsT=winb[:, k, m * 128:(m + 1) * 128], rhs=nT[:, k, :],
                                               start=(k == 0), stop=(k == 7))
                            return ins
                        P.op("tensor", mmf, reads=[nT.b], writes=[pb.b])
                        while pend:
                            pend.pop(0)()
                        if m < 8:
                            j = cnt["qk"]
                            cnt["qk"] += 1
                            sq_, sd_, sp_, qn_ = sq[j % 2], sd[j % 2], ssp[j % 2], qn[j % 3]
                            P.op("scalar", lambda e, pb=pb, sq_=sq_: e.activation(out=sq_[:], in_=pb[:], func=AF.Square),
                                 reads=[pb.b], writes=[sq_.b])
                            def rest(m=m, pb=pb, sq_=sq_, sd_=sd_, sp_=sp_, qn_=qn_, g0=g0):
                                P.op("tensor", lambda e, sq_=sq_, sp_=sp_: e.matmul(sp_[:], lhsT=blk1[:], rhs=sq_[:], start=True, stop=True),
                                     reads=[sq_.b, blk1.b], writes=[sp_.b])
                                P.op("scalar", lambda e, sp_=sp_, sd_=sd_: e.activation(out=sd_[:], in_=sp_[:], func=AF.Ln,
                                                                                     scale=1.0 / 64, bias=epsb[:, 0:1]),
                                     reads=[sp_.b, epsb.b], writes=[sd_.b])
                                P.op("scalar", lambda e, sd_=sd_: e.activation(out=sd_[:], in_=sd_[:], func=AF.Exp, scale=-0.5),
                                     reads=[sd_.b], writes=[sd_.b])
                                g_ = gq8 if m < 4 else gk1
                                P.op("vector", lambda e, pb=pb, sd_=sd_, qn_=qn_, g_=g_: e.scalar_tensor_tensor(
                                    out=qn_[:], in0=pb[:], scalar=g_[:, 0:1], in1=sd_[:], op0=ALU.mult, op1=ALU.mult),
                                    reads=[pb.b, sd_.b, g_.b], writes=[qn_.b])
                                dstv = QT_v if m < 4 else KT_v
                                dma(dstv[:, m % 4, g0:g0 + T], qn_[:], reads=[qn_.b], ds=qn_.ds)
                            pend.append(rest)
                        else:
                            j = cnt["of"]
                            cnt["of"] += 1
                            o_ = of[j % 3]
                            fnc = AF.Copy if m < 16 else AF.Gelu_apprx_tanh
                            P.op("scalar", lambda e, pb=pb, o_=o_, fnc=fnc: e.activation(out=o_[:], in_=pb[:], func=fnc),
                                 reads=[pb.b], writes=[o_.b])
                            dstv = XL_v if m < 16 else GG_v
                            dma(dstv[:, m % 4, g0:g0 + T], o_[:], reads=[o_.b], ds=o_.ds)
                    for sub in range(4):
                        pb = mmp[cnt["mm"] % 4]
                        cnt["mm"] += 1

                        def mmv(e, sub=sub, pb=pb, nT=nT):
                            ins = None
                            for k in range(8):
                                ins = e.matmul(pb[:], lhsT=nT[:, k, sub * 128:(sub + 1) * 128], rhs=winb[:, k, 1024:1536],
                                               start=(k == 0), stop=(k == 7))
                            return ins
                        P.op("tensor", mmv, reads=[nT.b], writes=[pb.b])
                        while pend:
                            pend.pop(0)()
                        v_ = vb[cnt["v"] % 2]
                        cnt["v"] += 1
                        P.op("vector", lambda e, pb=pb, v_=v_: e.tensor_copy(out=v_[:], in_=pb[:]), reads=[pb.b], writes=[v_.b])
                        dma(VV[g0 + sub * 128:g0 + (sub + 1) * 128, :], v_[:], reads=[v_.b], ds=v_.ds)
                P.barrier()
                P.emit()

        def lru_setup(st, dr):
            g = gate_d[dr]
            W = {}
            stg = sbuf(st, "gst_" + dr, [128, 4, 128], F32, True)
            for nm in ("wr", "wi"):
                W[nm] = sbuf(st, nm + "_" + dr, [128, 4, 128], BF16)
                dma(stg[:], g[nm], writes=[stg.b], ds=stg.ds)
                P.op("vector", lambda e, d_=W[nm]: e.tensor_copy(out=d_[:], in_=stg[:]), reads=[stg.b], writes=[W[nm].b])
            for nm in ("br", "bi", "lam"):
                W[nm] = sbuf(st, nm + "_" + dr, [128, 4], F32, True)
                dma(W[nm][:], g[nm], writes=[W[nm].b], ds=W[nm].ds)
            W["kap"] = sbuf(st, "kap_" + dr, [128, 4], F32)
            W["kap2"] = sbuf(st, "kap2_" + dr, [128, 4], F32)
            kap, kap2, lam = W["kap"], W["kap2"], W["lam"]
            P.op("scalar", lambda e: e.activation(out=kap[:], in_=lam[:], func=AF.Exp, scale=-1.0), reads=[lam.b], writes=[kap.b])
            P.op("scalar", lambda e: e.activation(out=kap[:], in_=kap[:], func=AF.Ln, bias=1.0), reads=[kap.b], writes=[kap.b])
            P.op("vector", lambda e: e.tensor_scalar(out=kap2[:], in0=kap[:], scalar1=-16.0, scalar2=None, op0=ALU.mult),
                 reads=[kap.b], writes=[kap2.b])
            P.op("vector", lambda e: e.tensor_scalar(out=kap[:], in0=kap[:], scalar1=-8.0, scalar2=None, op0=ALU.mult),
                 reads=[kap.b], writes=[kap.b])
            return W

        def lru_conv(xl, xc, xcb, cw, cb):
            for c in range(4):
                eng = ("vector", "gpsimd")[c % 2]
                P.op(eng, lambda e, c=c: e.tensor_scalar(out=xc[:, c, :], in0=xl[:, c, 0:T], scalar1=cw[:, c, 0:1],
                                                        scalar2=cb[:, c:c + 1], op0=ALU.mult, op1=ALU.add),
                     reads=[xl.b, cw.b, cb.b, xc.b], writes=[xc.b])
                yield
                for j in range(1, 4):
                    last = (j == 3)
                    P.op("vector", lambda e, c=c, j=j: e.scalar_tensor_tensor(out=xc[:, c, :], in0=xl[:, c, j:j + T],
                                                                        scalar=cw[:, c, j:j + 1], in1=xc[:, c, :],
                                                                        op0=ALU.mult, op1=ALU.add),
                         reads=[xl.b, cw.b, xc.b], writes=[xc.b])
                    yield
                P.op("scalar", lambda e, c=c: e.activation(out=xcb[:, c, :], in_=xc[:, c, :], func=AF.Copy),
                     reads=[xc.b, xcb.b], writes=[xcb.b])
                yield

        def lru_dir(W, xc, xcb, gp, rt, it, at, h, carry, reverse, first):
            for c in range(4):
                pr, pi = gp[(2 * c) % len(gp)], gp[(2 * c + 1) % len(gp)]
                P.op("tensor", lambda e, c=c, pr=pr: e.matmul(pr[:, 0:T], lhsT=W["wr"][:, c, :], rhs=xcb[:, c, :], start=True, stop=True),
                     reads=[xcb.b, W["wr"].b], writes=[pr.b])
                yield
                P.op("scalar", lambda e, c=c, pr=pr: e.activation(out=rt[:, c, :], in_=pr[:, 0:T], func=AF.Sigmoid, bias=W["br"][:, c:c + 1]),
                     reads=[pr.b, W["br"].b, rt.b], writes=[rt.b])
                yield
                P.op("tensor", lambda e, c=c, pi=pi: e.matmul(pi[:, 0:T], lhsT=W["wi"][:, c, :], rhs=xcb[:, c, :], start=True, stop=True),
                     reads=[xcb.b, W["wi"].b], writes=[pi.b])
                yield
                P.op("scalar", lambda e, c=c, pi=pi: e.activation(out=it[:, c, :], in_=pi[:, 0:T], func=AF.Sigmoid, bias=W["bi"][:, c:c + 1]),
                     reads=[pi.b, W["bi"].b, it.b], writes=[it.b])
                yield
            for c in range(4):
                P.op("scalar", lambda e, c=c: e.activation(out=at[:, c, :], in_=rt[:, c, :], func=AF.Exp, scale=W["kap"][:, c:c + 1]),
                     reads=[rt.b, W["kap"].b, at.b], writes=[at.b])
                yield
                P.op("scalar", lambda e, c=c: e.activation(out=rt[:, c, :], in_=rt[:, c, :], func=AF.Exp, scale=W["kap2"][:, c:c + 1]),
                     reads=[rt.b, W["kap2"].b], writes=[rt.b])
                yield
            for c in range(4):
                P.op("scalar", lambda e, c=c: e.activation(out=rt[:, c, :], in_=rt[:, c, :], func=AF.Sqrt, scale=-1.0, bias=1.0),
                     reads=[rt.b], writes=[rt.b])
                yield
                eng = ("vector", "gpsimd")[c % 2]
                P.op(eng, lambda e, c=c: e.tensor_tensor(out=it[:, c, :], in0=it[:, c, :], in1=xc[:, c, :], op=ALU.mult),
                     reads=[it.b, xc.b], writes=[it.b])
                yield
                P.op(eng, lambda e, c=c: e.tensor_tensor(out=it[:, c, :], in0=it[:, c, :], in1=rt[:, c, :], op=ALU.mult),
                     reads=[it.b, rt.b], writes=[it.b])
                yield
            for c in range(4):
                init = 0.0 if first else carry[:, c:c + 1]
                if reverse:
                    fn = lambda e, c=c, init=init: e.tensor_tensor_scan(out=h[:, c, ::-1], data0=at[:, c, ::-1], data1=it[:, c, ::-1],
                                                                        initial=init, op0=ALU.mult, op1=ALU.add)
                else:
                    fn = lambda e, c=c, init=init: e.tensor_tensor_scan(out=h[:, c, :], data0=at[:, c, :], data1=it[:, c, :],
                                                                        initial=init, op0=ALU.mult, op1=ALU.add)
                P.op("vector", fn, reads=[at.b, it.b, h.b, carry.b], writes=[h.b])
                yield
            col = 0 if reverse else T - 1
            P.op("gpsimd", lambda e, col=col: e.tensor_copy(out=carry[:], in_=h[:, :, col]), reads=[h.b, carry.b], writes=[carry.b])
            yield

        def load_xl(xl, s, b, g0, nt, q_=None, mid=None):
            lo = 0 if b == 0 else -1
            hi = T if b == nt - 1 else T + 2
            if b == 0:
                P.op("gpsimd", lambda e: e.memset(xl[:, :, 0:1], 0.0), reads=[xl.b], writes=[xl.b])
            if b == nt - 1:
                P.op("gpsimd", lambda e: e.memset(xl[:, :, T + 1:T + 3], 0.0), reads=[xl.b], writes=[xl.b])
            dma(xl[:, :, 1 + lo:1 + hi], XL_v[:, :, g0 + lo:g0 + hi], reads=[xl.b], writes=[xl.b], ds=xl.ds)
            if mid == "lo":
                P.op("vector", lambda e: e.tensor_scalar(out=xl[:, :, T + 1:T + 3], in0=xl[:, :, T + 1:T + 3], scalar1=flg[:, 0:1],
                                                         scalar2=None, op0=ALU.mult), reads=[xl.b, flg.b], writes=[xl.b])
            if mid == "hi":
                P.op("vector", lambda e: e.tensor_scalar(out=xl[:, :, 0:1], in0=xl[:, :, 0:1], scalar1=flg[:, 0:1],
                                                         scalar2=None, op0=ALU.mult), reads=[xl.b, flg.b], writes=[xl.b])

        def cut_carry(carry):
            P.op("vector", lambda e: e.tensor_scalar(out=carry[:], in0=carry[:], scalar1=flg[:, 0:1], scalar2=None, op0=ALU.mult),
                 reads=[carry.b, flg.b], writes=[carry.b])

        if 2 in phases:
            with ExitStack() as st:
                P.reset_dsem_pool()
                P.next_dsem = 4
                Wb = lru_setup(st, "b")
                cw = sbuf(st, "cw", [128, 4, 4], F32, True)
                cb = sbuf(st, "cb", [128, 4], F32, True)
                dma(cw[:], lcw_d, writes=[cw.b], ds=cw.ds)
                dma(cb[:], lcb_d, writes=[cb.b], ds=cb.ds)
                xl = [sbuf(st, "xl%d" % i, [128, 4, T + 3], F32, True) for i in range(2)]
                xc2 = [sbuf(st, "xc%d" % i, [128, 4, T], F32) for i in range(2)]
                xcb2 = [sbuf(st, "xcb%d" % i, [128, 4, T], BF16) for i in range(2)]
                rt = sbuf(st, "rt", [128, 4, T], F32)
                it = sbuf(st, "it", [128, 4, T], F32)
                at = sbuf(st, "at", [128, 4, T], F32)
                hh = [sbuf(st, "hh%d" % i, [128, 4, T], F32, True) for i in range(2)]
                carry = sbuf(st, "carry", [128, 4], F32)
                gp = [psum(st, "gp%d" % i, [128, 512], F32) for i in range(8)]
                order = []
                for lst in seq_tiles:
                    order += lst[::-1]
                load_xl(xl[0], *order[0])
                if len(order) > 1:
                    load_xl(xl[1], *order[1])
                for _ in lru_conv(xl[0], xc2[0], xcb2[0], cw, cb):
                    pass
                for i, (s, b, g0, nt, q_, mid) in enumerate(order):
                    h = hh[i % 2]
                    g_dir = lru_dir(Wb, xc2[i % 2], xcb2[i % 2], gp, rt, it, at, h, carry, True, b == nt - 1)
                    g_conv = None
                    if i + 1 < len(order):
                        g_conv = lru_conv(xl[(i + 1) % 2], xc2[(i + 1) % 2], xcb2[(i + 1) % 2], cw, cb)
                    alive = True
                    while alive:
                        alive = False
                        for g in (g_dir, g_conv):
                            if g is not None:
                                try:
                                    next(g)
                                    alive = True
                                except StopIteration:
                                    pass
                    if i + 2 < len(order):
                        load_xl(xl[i % 2], *order[i + 2])
                    if mid == "hi":
                        cut_carry(carry)
                    dma(HB_v[:, :, g0:g0 + T], h[:], reads=[h.b], ds=h.ds)
                P.barrier()
                P.emit()

        if 3 in phases:
            with ExitStack() as st:
                P.reset_dsem_pool()
                P.next_dsem = 4
                Wf = lru_setup(st, "f")
                cw = sbuf(st, "cw", [128, 4, 4], F32, True)
                cb = sbuf(st, "cb", [128, 4], F32, True)
                dma(cw[:], lcw_d, writes=[cw.b], ds=cw.ds)
                dma(cb[:], lcb_d, writes=[cb.b], ds=cb.ds)
                woutb = sbuf(st, "woutb", [128, 8, D], BF16)
                EI = sbuf(st, "EI", [128, 8, 11, 64], BF16)
                EF = sbuf(st, "EF", [128, 8, 15, 64], BF16)
                with ExitStack() as st2:
                    wstg = [sbuf(st2, "wstg%d" % i, [128, 8, 512], F32, True) for i in range(2)]
                    load_cast(woutb, w_out, 8, D, wstg)
                    for h in range(8):
                        w = wstg[h % 2]
                        wv_ = w[:].rearrange("p k n -> p (k n)")
                        dma(wv_[:, 0:11 * 64], TI_d[:, h].rearrange("p s c -> p (s c)"), writes=[w.b], ds=w.ds)
                        dma(wv_[:, 1024:1024 + 15 * 64], TF_d[:, h].rearrange("p s c -> p (s c)"), reads=[w.b], writes=[w.b], ds=w.ds)
                        P.op("scalar", lambda e, h=h, wv_=wv_: e.activation(out=EI[:, h].rearrange("p s c -> p (s c)"),
                                                                          in_=wv_[:, 0:11 * 64], func=AF.Exp),
                             reads=[w.b, EI.b], writes=[EI.b])
                        P.op("scalar", lambda e, h=h, wv_=wv_: e.activation(out=EF[:, h].rearrange("p s c -> p (s c)"),
                                                                          in_=wv_[:, 1024:1024 + 15 * 64], func=AF.Exp),
                             reads=[w.b, EF.b], writes=[EF.b])
                    P.barrier()
                    P.emit()
                g1row = sbuf(st, "g1row", [128, D], F32, True)
                xl = [sbuf(st, "xl%d" % i, [128, 4, T + 3], F32, True) for i in range(1)]
                slab = st.enter_context(nc.sbuf_tensor("slab3", [128, 8192], F32))

                def sview(name, ap, dma_=False):
                    tt = Tile.__new__(Tile)
                    tt.t = ap
                    tt.b = P.buf(name)
                    tt.ds = P.new_dsem() if dma_ else None
                    return tt
                xc = sview("xc", slab[:, 0:2048].rearrange("p (c t) -> p c t", c=4))
                rt = sview("rt", slab[:, 2048:4096].rearrange("p (c t) -> p c t", c=4))
                it = sview("it", slab[:, 4096:6144].rearrange("p (c t) -> p c t", c=4))
                at = sview("at", slab[:, 6144:8192].rearrange("p (c t) -> p c t", c=4))
                x1 = sview("x1", slab[:, 0:4096].rearrange("p (c t) -> p c t", c=4), True)
                xh = sview("xh", slab[:, 4096:6144].bitcast(BF16).rearrange("p (c t) -> p c t", c=4))
                n2 = sview("n2", slab[:, 6144:8192].bitcast(BF16).rearrange("p (c t) -> p c t", c=8), True)
                x1.b.al = [xc.b, rt.b]
                xc.b.al = [x1.b]
                rt.b.al = [x1.b]
                xh.b.al = [it.b]
                it.b.al = [xh.b]
                n2.b.al = [at.b]
                at.b.al = [n2.b]
                xcb = sbuf(st, "xcb", [128, 4, T], BF16)
                hh = [sbuf(st, "hh%d" % i, [128, 4, T], F32) for i in range(1)]
                carry = sbuf(st, "carry", [128, 4], F32)
                hbt = sbuf(st, "hbt", [128, 4, T], F32, True)
                ggt = sbuf(st, "ggt", [128, 4, T], F32, True)
                mixT = sbuf(st, "mixT", [128, 4, T], BF16)
                qT = sbuf(st, "qT", [128, 4, T], BF16, True)
                kT = sbuf(st, "kT", [128, 4, 2 * T], BF16, True)
                vt = sbuf(st, "vt", [128, 8, 512], BF16, True)
                ex = [sbuf(st, "ex%d" % i, [128, 2, T], BF16) for i in range(2)]
                pT = [sbuf(st, "pT%d" % i, [128, 8, 2, T], BF16) for i in range(2)]
                rc = sbuf(st, "rc", [128, T], F32)
                xr = [sbuf(st, "xr%d" % i, [128, D], F32, True) for i in range(2)]
                junk = sbuf(st, "junk", [128, D], BF16)
                ss = sbuf(st, "ss", [128, 4], F32)
                rs = sbuf(st, "rs", [128, 4], F32)
                pp = [psum(st, "pp%d" % i, [128, 1024], F32) for i in range(2)]
                nump = psum(st, "nump", [128, 512], F32)
                denp = psum(st, "denp", [128, 1024], F32)
                gp1 = psum(st, "gp1", [128, 512], F32)
                class V_:
                    pass

                def view(tile_, lo, hi, bf=False):
                    v = V_()
                    base = tile_.t
                    v.t = base
                    v.b = tile_.b
                    v.lo = lo
                    v.bf = bf
                    return v
                banks = []

                class Bank:
                    def __init__(self, tile_, lo):
                        self.tile = tile_
                        self.lo = lo
                        self.b = tile_.b

                    def __getitem__(self, idx):
                        a = idx[1].start or 0
                        bb = idx[1].stop if idx[1].stop is not None else 512
                        return self.tile.t[idx[0], self.lo + a:self.lo + bb]
                banks = [Bank(pp[0], 0), Bank(pp[0], 512), Bank(pp[1], 0), Bank(pp[1], 512),
                         Bank(denp, 0), Bank(denp, 512), Bank(nump, 0), Bank(gp1, 0)]
                tp = []
                for tile_ in (nump, gp1):
                    tt = Tile.__new__(Tile)
                    tt.t = tile_.t.bitcast(BF16)
                    tt.b = tile_.b
                    tt.ds = None
                    tp.append(tt)

                attB = sbuf(st, "attB", [128, 4, T], BF16)
                mixL = [sbuf(st, "mixL%d" % i, [128, 4, T], BF16) for i in range(2)]
                gbank = banks

                def lru_loads(i):
                    (s, b, g0, nt, q_, mid) = tiles[i]
                    load_xl(xl[0], s, b, g0, nt, q_, mid)
                    dma(hbt[:], HB_v[:, :, g0:g0 + T], writes=[hbt.b], ds=hbt.ds)
                    dma(ggt[:], GG_v[:, :, g0:g0 + T], writes=[ggt.b], ds=ggt.ds)

                def lru_stream(i):
                    (s, b, g0, nt, q_, mid) = tiles[i]
                    xl_ = xl[0]
                    yield from lru_conv(xl_, xc, xcb, cw, cb)
                    h = hh[0]
                    yield from lru_dir(Wf, xc, xcb, gbank, rt, it, at, h, carry, False, b == 0)
                    if mid == "lo":
                        cut_carry(carry)
                    mL = mixL[i % 2]
                    for c in range(4):
                        P.op("vector", lambda e, c=c, h=h: e.tensor_tensor(out=hbt[:, c, :], in0=hbt[:, c, :], in1=h[:, c, :], op=ALU.add),
                             reads=[hbt.b, h.b], writes=[hbt.b])
                        yield
                        P.op("vector", lambda e, c=c, mL=mL: e.tensor_tensor(out=mL[:, c, :], in0=hbt[:, c, :], in1=ggt[:, c, :], op=ALU.mult),
                             reads=[hbt.b, ggt.b, mL.b], writes=[mL.b])
                        yield

                cur_slot = {"s": -1}
                lru_loads(0)
                for i, (s, b, g0, nt, q_, mid) in enumerate(tiles):
                    if s != cur_slot["s"]:
                        cur_slot["s"] = s
                        dma(g1row[:], bcast_rows(GR[s:s + 1, :]), writes=[g1row.b], ds=g1row.ds)
                    gen = None
                    ld_top, ld_bot = (b == 0), (b == nt - 1)
                    jlo = 0 if ld_top else -2
                    jhi = 3 if ld_bot else 5
                    tok_lo = g0 + jlo * 128
                    tok_hi = g0 + (jhi + 1) * 128
                    dma(qT[:], QT_v[:, :, g0:g0 + T], writes=[qT.b], ds=qT.ds)
                    dma(kT[:, :, (jlo + 2) * 128:(jhi + 3) * 128], KT_v[:, :, tok_lo:tok_hi], writes=[kT.b], ds=kT.ds)
                    dma(vt[:, jlo + 2:jhi + 3, :], VV[tok_lo:tok_hi, :].rearrange("(j p) f -> p j f", p=128),
                        writes=[vt.b], ds=vt.ds)
                    for _ in lru_stream(i):
                        pass
                    if i + 1 < len(tiles):
                        lru_loads(i + 1)

                    def attention(top_t, bot_t, outT, o0, gen=None):
                        jlo = 0 if top_t else -2
                        jhi = 3 if bot_t else 5
                        CH = {}
                        for ii in range(4):
                            ch = [j for j in range(ii - 2, ii + 3) if jlo <= j <= jhi]
                            if top_t and ii == 0:
                                ch = [0, 1, 2, 3]
                            if bot_t and ii == 3:
                                ch = [0, 1, 2, 3]
                            CH[ii] = ch
                        def s_part(hpair):
                            pTt = pT[hpair % 2]
                            for jj in range(jlo, jhi + 1):
                                iis = [ii for ii in range(4) if jj in CH[ii]]
                                q0, q1 = 2 * iis[0], 2 * iis[-1] + 2
                                c0, c1 = q0 * 64, q1 * 64
                                ps = pp[(jj - jlo) % 2]
                                ex_ = ex[(jj - jlo) % 2]

                                def smm(e, jj=jj, ps=ps, c0=c0, c1=c1, hpair=hpair):
                                    ins = None
                                    for e2 in range(2):
                                        ins = e.matmul(ps[:, e2 * 512 + c0:e2 * 512 + c1],
                                                       lhsT=kT[e2 * 64:(e2 + 1) * 64, hpair, (jj + 2) * 128:(jj + 3) * 128],
                                                       rhs=qT[e2 * 64:(e2 + 1) * 64, hpair, c0:c1], start=True, stop=True)
                                    return ins
                                P.op("tensor", smm, reads=[kT.b, qT.b], writes=[ps.b])
                                psv = ps[:].rearrange("p (e c) -> p e c", e=2)
                                P.op("scalar", lambda e, psv=psv, ex_=ex_, c0=c0, c1=c1: e.activation(out=ex_[:, :, c0:c1], in_=psv[:, :, c0:c1], func=AF.Exp),
                                     reads=[ps.b, ex_.b], writes=[ex_.b])
                                segs = []
                                q = q0
                                while q < q1:
                                    rule = "F" if ((top_t and q < 4) or (bot_t and q > 4)) else "I"
                                    qe = q + 1
                                    while qe < q1 and (("F" if ((top_t and qe < 4) or (bot_t and qe > 4)) else "I") == rule):
                                        qe += 1
                                    segs.append((rule, q, qe))
                                    q = qe
                                for (rule, qa, qb) in segs:
                                    if rule == "I":
                                        s0 = 5 - 2 * jj + qa
                                        tab = EI
                                        assert 0 <= s0 and s0 + (qb - qa) <= 11, (s0, qa, qb, jj)
                                    else:
                                        s0 = 7 - 2 * jj + qa
                                        tab = EF
                                        assert 0 <= s0 and s0 + (qb - qa) <= 15, (s0, qa, qb, jj)
                                    nq = qb - qa
                                    P.op("vector", lambda e, tab=tab, s0=s0, nq=nq, qa=qa, qb=qb, ex_=ex_, pTt=pTt, jj=jj, hpair=hpair:
                                         e.tensor_tensor(out=pTt[:, jj + 2, :, qa * 64:qb * 64].rearrange("p e (q c) -> p e q c", c=64),
                                                         in0=ex_[:, :, qa * 64:qb * 64].rearrange("p e (q c) -> p e q c", c=64),
                                                         in1=tab[:, 2 * hpair:2 * hpair + 2, s0:s0 + nq, :], op=ALU.mult),
                                         reads=[ex_.b, tab.b, pTt.b], writes=[pTt.b])
                                if gen is not None:
                                    for _ in range(3):
                                        next(gen, None)
                        def pv_part(hpair):
                            pTt = pT[hpair % 2]
                            def pv(e, hpair=hpair, pTt=pTt, CH=CH):
                                ins = None
                                for ii in range(4):
                                    ch = CH[ii]
                                    for e2 in range(2):
                                        hd = 2 * hpair + e2
                                        for n_, jj in enumerate(ch):
                                            ins = e.matmul(nump[e2 * 64:(e2 + 1) * 64, ii * 128:(ii + 1) * 128],
                                                           lhsT=vt[:, jj + 2, hd * 64:(hd + 1) * 64],
                                                           rhs=pTt[:, jj + 2, e2, ii * 128:(ii + 1) * 128],
                                                           start=(n_ == 0), stop=(n_ == len(ch) - 1))
                                    for e2 in range(2):
                                        for n_, jj in enumerate(ch):
                                            ins = e.matmul(denp[:, e2 * 512 + ii * 128:e2 * 512 + (ii + 1) * 128],
                                                           lhsT=onesb[:], rhs=pTt[:, jj + 2, e2, ii * 128:(ii + 1) * 128],
                                                           start=(n_ == 0), stop=(n_ == len(ch) - 1))
                                return ins
                            P.op("tensor", pv, reads=[vt.b, pTt.b, onesb.b], writes=[nump.b, denp.b])
                            P.op("scalar", lambda e: e.activation(out=rc[0:64, :], in_=denp[0:64, 0:512], func=AF.Ln), reads=[denp.b, rc.b], writes=[rc.b])
                            P.op("scalar", lambda e: e.activation(out=rc[64:128, :], in_=denp[64:128, 512:1024], func=AF.Ln), reads=[denp.b, rc.b], writes=[rc.b])
                            P.op("scalar", lambda e: e.activation(out=rc[:], in_=rc[:], func=AF.Exp, scale=-1.0), reads=[rc.b], writes=[rc.b])
                            P.op("vector", lambda e, hpair=hpair, outT=outT, o0=o0: e.tensor_tensor(out=outT[:, o0 + hpair, :], in0=nump[:], in1=rc[:], op=ALU.mult),
                                 reads=[nump.b, rc.b, outT.b], writes=[outT.b])

                        s_part(0)
                        for hpair_ in range(1, 4):
                            s_part(hpair_)
                            pv_part(hpair_ - 1)
                        pv_part(3)

                    if mid is None:
                        attention(b == 0, b == nt - 1, mixT, 0, gen)
                    else:
                        attention(False, False, mixT, 0, gen)
                        attention(mid == "hi", mid == "lo", attB, 0)
                        P.op("vector", lambda e: e.tensor_scalar(out=mixT[:], in0=mixT[:], scalar1=flg[:, 0:1], scalar2=None, op0=ALU.mult),
                             reads=[mixT.b, flg.b], writes=[mixT.b])
                        P.op("vector", lambda e: e.scalar_tensor_tensor(out=mixT[:], in0=attB[:], scalar=flg[:, 1:2], in1=mixT[:],
                                                                       op0=ALU.mult, op1=ALU.add),
                             reads=[attB.b, flg.b, mixT.b], writes=[mixT.b])
                    if gen is not None:
                        for _ in gen:
                            pass
                    mLc = mixL[i % 2]
                    for sub in range(4):
                        xr_ = xr[sub % 2]
                        dma(xr_[:], xs[g0 + sub * 128:g0 + (sub + 1) * 128, :], writes=[xr_.b], ds=xr_.ds)
                        for half in range(2):
                            bk = banks[(sub * 2 + half) % 4]

                            def omm(e, sub=sub, half=half, bk=bk, mLc=mLc):
                                ins = None
                                for k in range(8):
                                    lt = mixT[:, k, sub * 128:(sub + 1) * 128] if k < 4 else mLc[:, k - 4, sub * 128:(sub + 1) * 128]
                                    ins = e.matmul(bk[:, 0:512], lhsT=lt,
                                                   rhs=woutb[:, k, half * 512:(half + 1) * 512], start=(k == 0), stop=(k == 7))
                                return ins
                            P.op("tensor", omm, reads=[mixT.b, mLc.b], writes=[bk.b])
                            P.op("vector", lambda e, sub=sub, half=half, bk=bk: e.tensor_tensor(
                                out=x1[:, sub, half * 512:(half + 1) * 512], in0=bk[:, 0:512], in1=g1row[:, half * 512:(half + 1) * 512], op=ALU.mult),
                                reads=[bk.b, g1row.b, x1.b], writes=[x1.b])
                            P.op("gpsimd", lambda e, sub=sub, half=half, xr_=xr_: e.tensor_tensor(
                                out=x1[:, sub, half * 512:(half + 1) * 512], in0=x1[:, sub, half * 512:(half + 1) * 512],
                                in1=xr_[:, half * 512:(half + 1) * 512], op=ALU.add),
                                reads=[x1.b, xr_.b], writes=[x1.b])
                    dma(X1_v[g0 // T], x1[:], reads=[x1.b], ds=x1.ds, eng="scalar")
                    P.op("vector", lambda e: e.memset(ss[:], 0.0), reads=[ss.b], writes=[ss.b])
                    rms_to_featmajor(x1, ss, rs, xh, junk, tp, n2, G2, mod, s, sh_off=24)
                    dma(N2_v[:, :, g0:g0 + T], n2[:], reads=[n2.b], ds=n2.ds, eng="scalar")
                P.barrier()
                P.emit()

        if 4 in phases:
            with ExitStack() as st:
                P.reset_dsem_pool()
                P.next_dsem = 4
                wupb = sbuf(st, "wupb", [128, 8, 5632], BF16)
                wdnb = sbuf(st, "wdnb", [128, 22, D], BF16)
                with ExitStack() as st2:
                    wstg = [sbuf(st2, "wstg%d" % i, [128, 8, 512], F32, True) for i in range(2)]
                    load_cast(wupb, w_up, 8, 5632, wstg, piece=512)
                    load_cast(wdnb, w_down, 22, D, wstg, piece=512)
                    P.barrier()
                    P.emit()
                fcw = sbuf(st, "fcw", [128, 44, 3], F32, True)
                fcb = sbuf(st, "fcb", [128, 44], F32, True)
                dma(fcw[:], fcw_d, writes=[fcw.b], ds=fcw.ds)
                dma(fcb[:], fcb_d, writes=[fcb.b], ds=fcb.ds)
                g2row = sbuf(st, "g2row", [128, D], F32, True)
                n2b = [sbuf(st, "n2h%d" % i, [128, 8, T], BF16, True) for i in range(2)]
                ub = [sbuf(st, "ub%d" % i, [128, T + 2], F32) for i in range(4)]
                cg = [sbuf(st, "cg%d" % i, [128, T], F32) for i in range(2)]
                cv = [sbuf(st, "cv%d" % i, [128, T], F32) for i in range(2)]
                sav = sbuf(st, "sav", [128, 44, 2], F32)
                actT = sbuf(st, "actT", [128, 22, T], BF16)
                xr = [sbuf(st, "xr%d" % i, [128, D], F32, True) for i in range(2)]
                for x_ in xr:
                    P.op("vector", lambda e, x_=x_: e.memset(x_[:], 0.0), writes=[x_.b])
                upp = [psum(st, "upp%d" % i, [128, 512], F32) for i in range(4)]
                dnp = [psum(st, "dnp%d" % i, [128, 512], F32) for i in range(4)]
                fF = sbuf(st, "fF", [1, D], F32)
                cur_slot = {"s": -1, "q": -1}

                def flush_token(tokL, tag, blend):
                    fl = sbuf(st, "fl" + tag, [128, 44, 1], F32)
                    flb = sbuf(st, "flb" + tag, [128, 22, 1], BF16)
                    fz = sbuf(st, "fz" + tag, [128, 44, 1], F32)
                    P.op("vector", lambda e: e.tensor_tensor(out=fl[:], in0=sav[:, :, 0:1], in1=fcw[:, :, 0:1], op=ALU.mult),
                         reads=[sav.b, fcw.b], writes=[fl.b])
                    P.op("vector", lambda e: e.tensor_tensor(out=fz[:], in0=sav[:, :, 1:2], in1=fcw[:, :, 1:2], op=ALU.mult),
                         reads=[sav.b, fcw.b], writes=[fz.b])
                    P.op("vector", lambda e: e.tensor_tensor(out=fl[:], in0=fl[:], in1=fz[:], op=ALU.add),
                         reads=[fl.b, fz.b], writes=[fl.b])
                    P.op("vector", lambda e: e.tensor_tensor(out=fl[:], in0=fl[:], in1=fcb[:].rearrange("p (m o) -> p m o", o=1), op=ALU.add),
                         reads=[fl.b, fcb.b], writes=[fl.b])
                    P.op("scalar", lambda e: e.activation(out=fl[:, 0:22, :], in_=fl[:, 0:22, :], func=AF.Gelu_apprx_tanh),
                         reads=[fl.b], writes=[fl.b])
                    P.op("vector", lambda e: e.tensor_tensor(out=flb[:], in0=fl[:, 0:22, :], in1=fl[:, 22:44, :], op=ALU.mult),
                         reads=[fl.b], writes=[flb.b])
                    x_ = xr[0]
                    dma(x_[0:1, :], X1[tokL:tokL + 1, :], writes=[x_.b], ds=x_.ds)
                    for half in range(2):
                        pb = dnp[half]

                        def fmm(e, half=half, pb=pb):
                            ins = None
                            for m in range(22):
                                ins = e.matmul(pb[0:1, :], lhsT=flb[:, m, :], rhs=wdnb[:, m, half * 512:(half + 1) * 512],
                                               start=(m == 0), stop=(m == 21))
                            return ins
                        P.op("tensor", fmm, reads=[flb.b], writes=[pb.b])
                        P.op("vector", lambda e, pb=pb, half=half: e.tensor_tensor(out=pb[0:1, :], in0=pb[0:1, :], in1=g2row[0:1, half * 512:(half + 1) * 512], op=ALU.mult),
                             reads=[pb.b, g2row.b], writes=[pb.b])
                        P.op("vector", lambda e, pb=pb, half=half: e.tensor_tensor(
                            out=x_[0:1, half * 512:(half + 1) * 512], in0=pb[0:1, :], in1=x_[0:1, half * 512:(half + 1) * 512], op=ALU.add),
                            reads=[pb.b, x_.b], writes=[x_.b])
                    if blend:
                        P.op("vector", lambda e: e.tensor_scalar(out=fF[:], in0=x_[0:1, :], scalar1=flg[0:1, 1:2], scalar2=None, op0=ALU.mult),
                             reads=[x_.b, flg.b], writes=[fF.b])
                    else:
                        dma(ys[tokL:tokL + 1, :], x_[0:1, :], reads=[x_.b], ds=x_.ds)

                for i, (s, b, g0, nt, q_, mid) in enumerate(tiles):
                    if q_ != cur_slot["q"]:
                        cur_slot["q"] = q_
                        P.op("vector", lambda e: e.memset(sav[:], 0.0), reads=[sav.b], writes=[sav.b])
                    if mid == "hi":
                        flush_token(g0 - 1, "m%d" % q_, True)
                        P.op("vector", lambda e: e.tensor_scalar(out=sav[:], in0=sav[:], scalar1=flg[:, 0:1], scalar2=None, op0=ALU.mult),
                             reads=[sav.b, flg.b], writes=[sav.b])
                    if s != cur_slot["s"]:
                        cur_slot["s"] = s
                        dma(g2row[:], bcast_rows(GR[NS + s:NS + s + 1, :]), writes=[g2row.b], ds=g2row.ds)
                    if i == 0:
                        dma(n2b[0][:], N2_v[:, :, g0:g0 + T], writes=[n2b[0].b], ds=n2b[0].ds)
                    if i + 1 < len(tiles):
                        gn = tiles[i + 1][2]
                        nn = n2b[(i + 1) % 2]
                        dma(nn[:], N2_v[:, :, gn:gn + T], writes=[nn.b], ds=nn.ds)
                    n2 = n2b[i % 2]
                    for m in range(22):
                        for which in range(2):
                            col = which * 2816 + m * 128
                            pb = upp[(2 * m + which) % 4]
                            u_ = ub[(2 * m + which) % 4]
                            ci = which * 22 + m

                            def umm(e, col=col, pb=pb, n2=n2):
                                ins = None
                                for k in range(8):
                                    ins = e.matmul(pb[:], lhsT=wupb[:, k, col:col + 128], rhs=n2[:, k, :], start=(k == 0), stop=(k == 7))
                                return ins
                            P.op("tensor", umm, reads=[n2.b], writes=[pb.b])
                            P.op("scalar", lambda e, u_=u_, pb=pb: e.activation(out=u_[:, 2:T + 2], in_=pb[:], func=AF.Copy),
                                 reads=[pb.b, u_.b], writes=[u_.b])
                            P.op("gpsimd", lambda e, u_=u_, ci=ci: e.tensor_copy(out=u_[:, 0:2], in_=sav[:, ci, :]),
                                 reads=[sav.b, u_.b], writes=[u_.b])
                            P.op("gpsimd", lambda e, u_=u_, ci=ci: e.tensor_copy(out=sav[:, ci, :], in_=u_[:, T:T + 2]),
                                 reads=[u_.b, sav.b], writes=[sav.b])
                            c_ = (cg if which == 0 else cv)[m % 2]
                            eng = "vector"
                            P.op("scalar", lambda e, u_=u_, c_=c_, ci=ci: e.activation(out=c_[:], in_=u_[:, 0:T], func=AF.Identity,
                                                                                      scale=fcw[:, ci, 0:1], bias=fcb[:, ci:ci + 1]),
                                 reads=[u_.b, fcw.b, fcb.b], writes=[c_.b])
                            for j in (1, 2):
                                P.op(eng, lambda e, u_=u_, c_=c_, ci=ci, j=j: e.scalar_tensor_tensor(
                                    out=c_[:], in0=u_[:, j:j + T], scalar=fcw[:, ci, j:j + 1], in1=c_[:], op0=ALU.mult, op1=ALU.add),
                                    reads=[u_.b, fcw.b, c_.b], writes=[c_.b])
                        g_, v_ = cg[m % 2], cv[m % 2]
                        P.op("scalar", lambda e, g_=g_: e.activation(out=g_[:], in_=g_[:], func=AF.Gelu_apprx_tanh), reads=[g_.b], writes=[g_.b])
                        P.op("vector", lambda e, g_=g_, v_=v_, m=m: e.tensor_tensor(out=actT[:, m, :], in0=g_[:], in1=v_[:], op=ALU.mult),
                             reads=[g_.b, v_.b, actT.b], writes=[actT.b])
                    last = (b == nt - 1)
                    for sub in range(4):
                        tok0 = g0 - 1 + sub * 128
                        p0 = 1 if (b == 0 and sub == 0) else 0
                        x_ = xr[sub % 2]
                        dma(x_[p0:128, :], X1[tok0 + p0:tok0 + 128, :], writes=[x_.b], ds=x_.ds)
                        for half in range(2):
                            pb = dnp[(sub * 2 + half) % 4]

                            def dmm(e, sub=sub, half=half, pb=pb):
                                ins = None
                                for m in range(22):
                                    ins = e.matmul(pb[:], lhsT=actT[:, m, sub * 128:(sub + 1) * 128],
                                                   rhs=wdnb[:, m, half * 512:(half + 1) * 512], start=(m == 0), stop=(m == 21))
                                return ins
                            P.op("tensor", dmm, reads=[actT.b], writes=[pb.b])
                            P.op("vector", lambda e, pb=pb, half=half: e.tensor_tensor(out=pb[:], in0=pb[:], in1=g2row[:, half * 512:(half + 1) * 512], op=ALU.mult),
                                 reads=[pb.b, g2row.b], writes=[pb.b])
                            P.op("vector", lambda e, pb=pb, half=half, x_=x_: e.tensor_tensor(
                                out=x_[:, half * 512:(half + 1) * 512], in0=pb[:], in1=x_[:, half * 512:(half + 1) * 512], op=ALU.add),
                                reads=[pb.b, x_.b], writes=[x_.b])
                        if mid == "hi" and sub == 0:
                            P.op("vector", lambda e, x_=x_: e.scalar_tensor_tensor(out=x_[0:1, :], in0=x_[0:1, :], scalar=flg[0:1, 0:1], in1=fF[:],
                                                                                  op0=ALU.mult, op1=ALU.add),
                                 reads=[x_.b, flg.b, fF.b], writes=[x_.b])
                        dma(ys[tok0 + p0:tok0 + 128, :], x_[p0:128, :], reads=[x_.b], ds=x_.ds)
                    if last:
                        flush_token(g0 + T - 1, "e%d" % q_, False)
                P.barrier()
                P.emit()
    return nc


def _fm(v, nchunk):
    return np.ascontiguousarray(np.asarray(v, np.float32).reshape(nchunk, 128).T)


def _bias_tables(rpb):
    rpb = np.asarray(rpb, np.float32)
    kc = np.arange(64)[:, None]
    c = np.arange(64)[None, :]
    cstart = np.clip(c - 8, 0, 48)
    colok = (kc >= cstart) & (kc < cstart + 16)
    dc = np.clip(kc - c + 15, 0, 30)
    TI = np.full((2, 64, 8, 11, 64), NEG, np.float32)
    TF = np.full((2, 64, 8, 15, 64), NEG, np.float32)
    for a in range(2):
        for s in range(11):
            dr = 12 - s + a
            if 3 <= dr <= 10:
                g = rpb[:, dr][:, dc]
                TI[a, :, :, s, :] = np.where(colok[:, None, :], g.transpose(1, 0, 2), NEG)
        for s in range(15):
            dr = 14 - s + a
            if 0 <= dr <= 14:
                g = rpb[:, dr][:, dc]
                TF[a, :, :, s, :] = np.where(colok[:, None, :], g.transpose(1, 0, 2), NEG)
    return TI.reshape(128, 8, 11, 64), TF.reshape(128, 8, 15, 64)


def _blockdiag(w):
    w = np.asarray(w, np.float32)
    out = np.zeros((128, 4, 128), np.float32)
    for h in range(8):
        c, e = h // 2, h % 2
        out[e * 64:(e + 1) * 64, c, e * 64:(e + 1) * 64] = w[h]
    return out


def make_common(inp):
    TI, TF = _bias_tables(inp["rpb"][0])
    cm = dict(
        w_ada=np.ascontiguousarray(inp["w_ada"][0]), badaT=_fm(inp["b_ada"][0], 48),
        ln1T=_fm(inp["ln1_g"][0], 8), ln2T=_fm(inp["ln2_g"][0], 8),
        w_in=np.ascontiguousarray(inp["w_in"][0]), w_out=np.ascontiguousarray(inp["w_out"][0]),
        w_up=np.ascontiguousarray(inp["w_up"][0]), w_down=np.ascontiguousarray(inp["w_down"][0]),
        gq=np.tile(np.asarray(inp["q_norm_g"][0], np.float32), 2).reshape(128, 1),
        gk=np.tile(np.asarray(inp["k_norm_g"][0], np.float32), 2).reshape(128, 1),
        TI=TI, TF=TF,
        lcw=np.ascontiguousarray(np.asarray(inp["lru_conv_w"][0], np.float32).reshape(4, 4, 128).transpose(2, 1, 0)),
        lcb=_fm(inp["lru_conv_b"][0], 4),
        fcw=np.ascontiguousarray(np.asarray(inp["ffn_conv_w"][0], np.float32).reshape(3, 44, 128).transpose(2, 1, 0)),
        fcb=_fm(inp["ffn_conv_b"][0], 44),
        ident=np.eye(128, dtype=np.float32),
    )
    for dr in ("f", "b"):
        cm["wr_" + dr] = _blockdiag(inp["w_r_" + dr][0])
        cm["wi_" + dr] = _blockdiag(inp["w_i_" + dr][0])
        cm["br_" + dr] = _fm(inp["b_r_" + dr][0], 4)
        cm["bi_" + dr] = _fm(inp["b_i_" + dr][0], 4)
        cm["lam_" + dr] = _fm(inp["lam_" + dr][0], 4)
    return cm


def core_map(cm, xs_list, c_list, flag):
    m = dict(cm)
    m["xs"] = np.ascontiguousarray(np.concatenate([np.asarray(x, np.float32) for x in xs_list], axis=0))
    cs = np.stack([np.asarray(c, np.float32) for c in c_list], axis=0)
    m["cT"] = np.ascontiguousarray(cs.reshape(len(c_list), 8, 128).transpose(2, 1, 0))
    fl = np.empty((128, 2), np.float32)
    fl[:, 0] = float(flag)
    fl[:, 1] = 1.0 - float(flag)
    m["flag"] = fl
    return m


SEQ_SPEC = ((4096, False), (8192, True))
_NC_CACHE = {}


def kernel(**inp):
    xp = np.asarray(inp["x_prompt"], np.float32)
    xsm = np.asarray(inp["x_sample"], np.float32)
    cp = np.asarray(inp["c_prompt"], np.float32)
    csm = np.asarray(inp["c_sample"], np.float32)
    if SEQ_SPEC not in _NC_CACHE:
        _NC_CACHE[SEQ_SPEC] = build_program(SEQ_SPEC)
    nc = _NC_CACHE[SEQ_SPEC]
    cm = make_common(inp)
    zx = np.zeros((4096, D), np.float32)
    zc = np.zeros((D,), np.float32)
    plan = []
    for core in range(4):
        plan.append((3 * core, 3 * core + 1, 3 * core + 2))
    plan.append((12, 13, None))
    plan.append((14, 15, None))
    in_maps = []
    for core in range(8):
        if core < 6:
            ia, ib, ic = plan[core]
            xs_l = [xp[ia], xp[ib], xp[ic] if ic is not None else zx]
            c_l = [cp[ia], cp[ib], cp[ic] if ic is not None else zc]
            in_maps.append(core_map(cm, xs_l, c_l, 0.0))
        else:
            k = core - 6
            in_maps.append(core_map(cm, [zx, xsm[k]], [zc, csm[k], csm[k]], 1.0))
    res = run_bass_kernel_spmd(nc, in_maps, core_ids=list(range(8)))
    yp = np.empty_like(xp)
    ysm = np.empty_like(xsm)
    for core in range(8):
        y = res.results[core]["ys"]
        if core < 6:
            ia, ib, ic = plan[core]
            yp[ia] = y[0:4096]
            yp[ib] = y[4096:8192]
            if ic is not None:
                yp[ic] = y[8192:12288]
        else:
            ysm[core - 6] = y[4096:12288]
    return (yp, ysm)
```

```python
import os
import numpy as np
from contextlib import ExitStack
import concourse.bass as bass
import concourse.mybir as mybir
from concourse.bass_utils import run_bass_kernel_spmd

F32 = mybir.dt.float32
BF16 = mybir.dt.bfloat16
AF = mybir.ActivationFunctionType
ALU = mybir.AluOpType

ENGS = ("sync", "scalar", "vector", "gpsimd", "tensor")
D = 1024
T = 512
NEG = -30000.0


class Buf:
    __slots__ = ("name", "w", "r", "al")

    def __init__(self, name):
        self.name = name
        self.w = []
        self.r = []
        self.al = []


class Prog:
    def __init__(self, nc, stack, n_dsem=48):
        self.nc = nc
        self.stack = stack
        self.ops = {e: [] for e in ENGS}
        self.cnt = {e: 0 for e in ENGS}
        self.waited = {e: {} for e in ENGS}
        self.esem = {e: stack.enter_context(nc.semaphore("es_" + e)) for e in ENGS}
        self.dsems = [[stack.enter_context(nc.semaphore("ds%d" % i)), 0] for i in range(n_dsem)]
        self.next_dsem = 0
        self.all_bufs = []

    def buf(self, name, dma=False):
        b = Buf(name)
        self.all_bufs.append(b)
        return b

    def new_dsem(self):
        i = self.next_dsem
        self.next_dsem += 1
        assert i < len(self.dsems), "out of dma sems"
        return i

    def reset_dsem_pool(self):
        self.next_dsem = 0

    def _sem_handle(self, key):
        return self.esem[key] if isinstance(key, str) else self.dsems[key][0]

    def _collect(self, eng, reads, writes):
        need = {}
        for b0 in reads:
            for b in [b0] + b0.al:
                for (k, v) in b.w:
                    need[k] = max(need.get(k, 0), v)
        for b0 in writes:
            for b in [b0] + b0.al:
                for (k, v) in b.w:
                    need[k] = max(need.get(k, 0), v)
                for (k, v) in b.r:
                    need[k] = max(need.get(k, 0), v)
        waits = []
        wd = self.waited[eng]
        for k, v in need.items():
            if not isinstance(k, str):
                v = max(v, self.dsems[k][1])
            if wd.get(k, 0) < v:
                wd[k] = v
                waits.append((k, v))
        return waits

    def op(self, eng, fn, reads=(), writes=(), dsem=None):
        waits = self._collect(eng, reads, writes)
        if eng == "tensor":
            waits = [(k, v) for (k, v) in waits if k != "tensor"]
        if dsem is None:
            self.cnt[eng] += 1
            tok = (eng, self.cnt[eng])
            inc = ("e", eng)
        else:
            self.dsems[dsem][1] += 16
            tok = (dsem, self.dsems[dsem][1])
            inc = ("d", dsem)
        self.ops[eng].append((waits, fn, inc))
        for b in writes:
            b.w = [tok]
            b.r = []
        for b in reads:
            if b not in writes:
                b.r.append(tok)
                if len(b.r) > 64:
                    b.r = b.r[-64:]
        return tok

    def barrier(self):
        final = {}
        for e in ENGS:
            if self.cnt[e]:
                final[e] = self.cnt[e]
        for i, (h, c) in enumerate(self.dsems):
            if c:
                final[i] = c
        for e in ENGS:
            waits = []
            for k, v in final.items():
                if self.waited[e].get(k, 0) < v:
                    self.waited[e][k] = v
                    waits.append((k, v))
            if waits:
                self.ops[e].append((waits, None, None))
        for b in self.all_bufs:
            b.w = []
            b.r = []

    def emit(self):
        nc = self.nc
        with nc.Block() as block:
            def run(engname):
                def body(e):
                    for (waits, fn, inc) in self.ops[engname]:
                        for (k, v) in waits:
                            e.wait_ge(self._sem_handle(k), v)
                        if fn is None:
                            continue
                        ins = fn(e)
                        if inc[0] == "e":
                            ins.then_inc(self.esem[inc[1]], 1)
                        else:
                            ins.then_inc(self.dsems[inc[1]][0], 16)
                return body
            block.sync(run("sync"))
            block.scalar(run("scalar"))
            block.vector(run("vector"))
            block.gpsimd(run("gpsimd"))
            block.tensor(run("tensor"))
        for e in ENGS:
            self.ops[e] = []


class Tile:
    def __init__(self, P, t, name, dma=False):
        self.t = t
        self.b = P.buf(name)
        self.ds = P.new_dsem() if dma else None

    def __getitem__(self, idx):
        return self.t[idx]


def build_program(seq_spec, debug=False, phases=(0, 1, 2, 3, 4)):
    seq_spec = [(int(L), bool(m)) for (L, m) in seq_spec]
    NS = sum(2 if m else 1 for (L, m) in seq_spec)
    NTOK = sum(L for (L, m) in seq_spec)
    nc = bass.Bass("TRN2", target_bir_lowering=False)

    def din(name, shape, dt=F32):
        return nc.dram_tensor(name, list(shape), dt, kind="ExternalInput").ap()

    skind = "ExternalOutput" if debug else "Internal"

    def dscr(name, shape, dt=F32):
        return nc.dram_tensor(name, list(shape), dt, kind=skind).ap()

    xs = din("xs", [NTOK, D])
    cT = din("cT", [128, 8, NS])
    w_ada = din("w_ada", [D, 6 * D])
    badaT = din("badaT", [128, 48])
    ln1T = din("ln1T", [128, 8])
    ln2T = din("ln2T", [128, 8])
    w_in = din("w_in", [D, 2560])
    w_out = din("w_out", [D, D])
    w_up = din("w_up", [D, 5632])
    w_down = din("w_down", [2816, D])
    gq_d = din("gq", [128, 1])
    gk_d = din("gk", [128, 1])
    TI_d = din("TI", [128, 8, 11, 64])
    TF_d = din("TF", [128, 8, 15, 64])
    lcw_d = din("lcw", [128, 4, 4])
    lcb_d = din("lcb", [128, 4])
    gate_d = {}
    for dr in ("f", "b"):
        gate_d[dr] = dict(wr=din("wr_" + dr, [128, 4, 128]), wi=din("wi_" + dr, [128, 4, 128]),
                          br=din("br_" + dr, [128, 4]), bi=din("bi_" + dr, [128, 4]),
                          lam=din("lam_" + dr, [128, 4]))
    fcw_d = din("fcw", [128, 44, 3])
    fcb_d = din("fcb", [128, 44])
    ident_d = din("ident", [128, 128])
    flag_d = din("flag", [128, 2])
    ys = nc.dram_tensor("ys", [NTOK, D], F32, kind="ExternalOutput").ap()

    QT = dscr("QT", [512, NTOK], BF16)
    KT = dscr("KT", [512, NTOK], BF16)
    VV = dscr("VV", [NTOK, 512], BF16)
    XL = dscr("XL", [512, NTOK])
    GG = dscr("GG", [512, NTOK])
    HB = dscr("HB", [512, NTOK])
    X1 = dscr("X1", [NTOK, D])
    N2T = dscr("N2T", [D, NTOK], BF16)
    GR = dscr("GR", [2 * NS, D])

    def bcast_rows(ap_row, nparts=128):
        t = ap_row.tensor
        return bass.AP(t, ap_row.offset, [[0, nparts]] + [list(x) for x in ap_row.ap[1:]])

    with ExitStack() as top:
        P = Prog(nc, top)

        def dma(out, in_, reads=(), writes=(), ds=None, eng="sync", slow=False):
            if slow:
                fn = lambda e, o=out, i=in_: e.dma_start(out=o, in_=i, allow_slow_non_contiguous=True)
            else:
                fn = lambda e, o=out, i=in_: e.dma_start(out=o, in_=i)
            return P.op(eng, fn, reads=reads, writes=writes, dsem=ds)

        rr = {"i": 0}

        def any_eng():
            rr["i"] += 1
            return ("vector", "gpsimd")[rr["i"] % 2]

        uid = {"n": 0}

        def sbuf(st, name, shape, dt, dma_=False):
            uid["n"] += 1
            name = "s%d_%s" % (uid["n"], name)
            return Tile(P, st.enter_context(nc.sbuf_tensor(name, list(shape), dt)), name, dma=dma_)

        def psum(st, name, shape, dt):
            uid["n"] += 1
            name = "p%d_%s" % (uid["n"], name)
            return Tile(P, st.enter_context(nc.psum_tensor(name, list(shape), dt)), name)

        identb = sbuf(top, "identb", [128, 128], BF16)
        onesb = sbuf(top, "onesb", [128, 128], BF16)
        blk1 = sbuf(top, "blk1", [128, 128], BF16)
        epsb = sbuf(top, "epsb", [128, 1], F32)
        mod = sbuf(top, "mod", [128, 48, NS], F32)
        G1 = sbuf(top, "G1", [128, 8, NS], F32)
        G2 = sbuf(top, "G2", [128, 8, NS], F32)
        gq8 = sbuf(top, "gq8", [128, 1], F32, True)
        flg = sbuf(top, "flg", [128, 2], F32, True)
        gk1 = sbuf(top, "gk1", [128, 1], F32, True)

        with ExitStack() as st:
            P.reset_dsem_pool()
            P.next_dsem = 4
            idf = sbuf(st, "idf", [128, 128], F32, True)
            cs = sbuf(st, "cs", [128, 8, NS], F32, True)
            scs = sbuf(st, "scs", [128, 8, NS], F32)
            bada = sbuf(st, "bada", [128, 48], F32, True)
            l1 = sbuf(st, "l1", [128, 8], F32, True)
            l2 = sbuf(st, "l2", [128, 8], F32, True)
            wst = [sbuf(st, "wst%d" % i, [128, 8, 512], F32, True) for i in range(2)]
            pmod = psum(st, "pmod", [128, 512], F32)
            tmp = sbuf(st, "tmp0", [128, 8, NS], F32)

            dma(idf[:], ident_d, writes=[idf.b], ds=idf.ds)
            P.op("vector", lambda e: e.tensor_copy(out=identb[:], in_=idf[:]), reads=[idf.b], writes=[identb.b])
            P.op("vector", lambda e: e.memset(onesb[:], 1.0), writes=[onesb.b])
            P.op("vector", lambda e: e.memset(blk1[:], 0.0), writes=[blk1.b])
            P.op("vector", lambda e: e.memset(blk1[0:64, 0:64], 1.0), reads=[blk1.b], writes=[blk1.b])
            P.op("vector", lambda e: e.memset(blk1[64:128, 64:128], 1.0), reads=[blk1.b], writes=[blk1.b])
            P.op("vector", lambda e: e.memset(epsb[:], 1e-6), writes=[epsb.b])
            dma(gq8[:], gq_d, writes=[gq8.b], ds=gq8.ds)
            dma(flg[:], flag_d, writes=[flg.b], ds=flg.ds)
            dma(gk1[:], gk_d, writes=[gk1.b], ds=gk1.ds)
            P.op("vector", lambda e: e.tensor_scalar(out=gq8[:], in0=gq8[:], scalar1=0.125, scalar2=None, op0=ALU.mult),
                 reads=[gq8.b], writes=[gq8.b])
            dma(cs[:], cT, writes=[cs.b], ds=cs.ds)
            dma(bada[:], badaT, writes=[bada.b], ds=bada.ds)
            dma(l1[:], ln1T, writes=[l1.b], ds=l1.ds)
            dma(l2[:], ln2T, writes=[l2.b], ds=l2.ds)
            P.op("scalar", lambda e: e.activation(out=scs[:], in_=cs[:], func=AF.Silu), reads=[cs.b], writes=[scs.b])
            wv = w_ada.rearrange("(k p) n -> p k n", p=128)
            for blk in range(12):
                w = wst[blk % 2]
                dma(w[:], wv[:, :, blk * 512:(blk + 1) * 512], writes=[w.b], ds=w.ds)

                def mm(e, w=w, blk=blk):
                    ins = None
                    for m in range(4):
                        for k in range(8):
                            ins = e.matmul(pmod[:, (blk * 4 + m) * NS:(blk * 4 + m + 1) * NS],
                                           lhsT=w[:, k, m * 128:(m + 1) * 128], rhs=scs[:, k, :],
                                           start=(k == 0), stop=(k == 7))
                    return ins
                P.op("tensor", mm, reads=[w.b, scs.b], writes=[pmod.b])
            pm3 = pmod[:, 0:48 * NS].rearrange("p (m s) -> p m s", s=NS)
            for s in range(NS):
                P.op("vector", lambda e, s=s: e.tensor_tensor(out=mod[:, :, s], in0=pm3[:, :, s], in1=bada[:], op=ALU.add),
                     reads=[pmod.b, bada.b, mod.b], writes=[mod.b])
            for (G, ln, o) in ((G1, l1, 8), (G2, l2, 32)):
                for s in range(NS):
                    P.op("vector", lambda e, s=s, o=o: e.tensor_scalar(out=tmp[:, :, s], in0=mod[:, o:o + 8, s], scalar1=1.0,
                                                                     scalar2=None, op0=ALU.add),
                         reads=[mod.b, tmp.b], writes=[tmp.b])
                    P.op("vector", lambda e, s=s, G=G, ln=ln: e.tensor_tensor(out=G[:, :, s], in0=tmp[:, :, s], in1=ln[:], op=ALU.mult),
                         reads=[tmp.b, ln.b, G.b], writes=[G.b])
            for gi, o in ((0, 16), (1, 40)):
                for s in range(NS):
                    dst = GR[gi * NS + s:gi * NS + s + 1, :].rearrange("o (k p) -> p (o k)", p=128)
                    dma(dst, mod[:, o:o + 8, s], reads=[mod.b], ds=3, slow=True)
            P.barrier()
            P.emit()

        def load_cast(dst, src, K, N, wstg, piece=512):
            sv = src.rearrange("(k p) n -> p k n", p=128)
            i = 0
            for k0 in range(0, K, 8):
                k1 = min(K, k0 + 8)
                for n0 in range(0, N, piece):
                    n1 = min(N, n0 + piece)
                    w = wstg[i % len(wstg)]
                    i += 1
                    dma(w[:, 0:k1 - k0, 0:n1 - n0], sv[:, k0:k1, n0:n1], writes=[w.b], ds=w.ds)
                    eng = ("vector", "gpsimd", "scalar")[i % 3]
                    if eng == "scalar":
                        P.op(eng, lambda e, w=w, k0=k0, k1=k1, n0=n0, n1=n1: e.activation(
                            out=dst[:, k0:k1, n0:n1], in_=w[:, 0:k1 - k0, 0:n1 - n0], func=AF.Copy),
                            reads=[w.b], writes=[])
                    else:
                        P.op(eng, lambda e, w=w, k0=k0, k1=k1, n0=n0, n1=n1: e.tensor_copy(
                            out=dst[:, k0:k1, n0:n1], in_=w[:, 0:k1 - k0, 0:n1 - n0]),
                            reads=[w.b], writes=[])

        def rms_to_featmajor(xt, ss, rs, xh, junk, tp, nT, Gm, shm, s, sh_off=0, ev_engs=("scalar", "vector")):
            for sub in range(4):
                P.op("scalar", lambda e, sub=sub: e.activation(out=junk[:], in_=xt[:, sub, :], func=AF.Square,
                                                              accum_out=ss[:, sub:sub + 1]),
                     reads=[xt.b, ss.b], writes=[junk.b, ss.b])
            P.op("scalar", lambda e: e.activation(out=rs[:], in_=ss[:], func=AF.Sqrt, scale=1.0 / D, bias=epsb[:, 0:1]),
                 reads=[ss.b, epsb.b], writes=[rs.b])
            P.op("vector", lambda e: e.reciprocal(out=rs[:], in_=rs[:]), reads=[rs.b], writes=[rs.b])
            for sub in range(4):
                if sub % 2 == 0:
                    P.op("vector", lambda e, sub=sub: e.tensor_scalar(out=xh[:, sub, :], in0=xt[:, sub, :], scalar1=rs[:, sub:sub + 1],
                                                                     scalar2=None, op0=ALU.mult),
                         reads=[xt.b, rs.b, xh.b], writes=[xh.b])
                else:
                    P.op("scalar", lambda e, sub=sub: e.activation(out=xh[:, sub, :], in_=xt[:, sub, :], func=AF.Copy,
                                                                  scale=rs[:, sub:sub + 1]),
                         reads=[xt.b, rs.b, xh.b], writes=[xh.b])
            for kk in range(4):
                tpb = tp[kk % 2]

                def tr(e, kk=kk, tpb=tpb):
                    ins = None
                    for j in range(2):
                        k = 2 * kk + j
                        for sub in range(4):
                            ins = e.transpose(out=tpb[:, j * 512 + sub * 128: j * 512 + (sub + 1) * 128],
                                              in_=xh[:, sub, k * 128:(k + 1) * 128], identity=identb[:])
                    return ins
                P.op("tensor", tr, reads=[xh.b, identb.b], writes=[tpb.b])
                for j in range(2):
                    k = 2 * kk + j
                    eng = ev_engs[j % len(ev_engs)]
                    if eng == "scalar":
                        P.op("scalar", lambda e, k=k, j=j, tpb=tpb: e.activation(
                            out=nT[:, k, :], in_=tpb[:, j * 512:(j + 1) * 512], func=AF.Identity,
                            scale=Gm[:, k, s:s + 1], bias=shm[:, sh_off + k, s:s + 1]),
                            reads=[tpb.b, Gm.b, mod.b, nT.b], writes=[nT.b])
                    else:
                        P.op(eng, lambda e, k=k, j=j, tpb=tpb: e.tensor_scalar(
                            out=nT[:, k, :], in0=tpb[:, j * 512:(j + 1) * 512], scalar1=Gm[:, k, s:s + 1],
                            scalar2=shm[:, sh_off + k, s:s + 1], op0=ALU.mult, op1=ALU.add),
                            reads=[tpb.b, Gm.b, mod.b, nT.b], writes=[nT.b])

        xs_v = xs.rearrange("(n s p) d -> n p s d", s=4, p=128)
        ys_v = ys.rearrange("(n s p) d -> n p s d", s=4, p=128)
        X1_v = X1.rearrange("(n s p) d -> n p s d", s=4, p=128)
        QT_v = QT.rearrange("(c p) t -> p c t", p=128)
        KT_v = KT.rearrange("(c p) t -> p c t", p=128)
        XL_v = XL.rearrange("(c p) t -> p c t", p=128)
        GG_v = GG.rearrange("(c p) t -> p c t", p=128)
        HB_v = HB.rearrange("(c p) t -> p c t", p=128)
        N2_v = N2T.rearrange("(c p) t -> p c t", p=128)

        tiles = []
        seq_tiles = []
        g_ = 0
        m_ = 0
        for q, (L, has_mid) in enumerate(seq_spec):
            nt = L // T
            lst = []
            for b in range(nt):
                mid = None
                ms = m_
                if has_mid:
                    ms = m_ + (1 if b >= nt // 2 else 0)
                    if b == nt // 2 - 1:
                        mid = "lo"
                    elif b == nt // 2:
                        mid = "hi"
                lst.append((ms, b, g_ + b * T, nt, q, mid))
            tiles += lst
            seq_tiles.append(lst)
            g_ += L
            m_ += 2 if has_mid else 1

        if 1 in phases:
            with ExitStack() as st:
                P.reset_dsem_pool()
                P.next_dsem = 4
                winb = sbuf(st, "winb", [128, 8, 2560], BF16)
                wstg = [sbuf(st, "wstg%d" % i, [128, 8, 512], F32, True) for i in range(2)]
                load_cast(winb, w_in, 8, 2560, wstg)
                P.barrier()
                xt = [sbuf(st, "xt%d" % i, [128, 4, D], F32, True) for i in range(2)]
                junk = sbuf(st, "junk", [128, D], BF16)
                xh2 = [sbuf(st, "xh%d" % i, [128, 4, D], BF16) for i in range(2)]
                nT2 = [sbuf(st, "nT%d" % i, [128, 8, T], BF16) for i in range(2)]
                ss2 = [sbuf(st, "ssb%d" % i, [128, 4], F32) for i in range(2)]
                rs2 = [sbuf(st, "rsb%d" % i, [128, 4], F32) for i in range(2)]
                tpf = [psum(st, "tp%d" % i, [128, 512], F32) for i in range(2)]
                tp = [Tile.__new__(Tile) for _ in range(2)]
                for i in range(2):
                    tp[i].t = tpf[i].t.bitcast(BF16)
                    tp[i].b = tpf[i].b
                    tp[i].ds = None
                mmp = [psum(st, "mm%d" % i, [128, 512], F32) for i in range(4)]
                ssp = [psum(st, "ssp%d" % i, [128, 512], F32) for i in range(2)]
                sq = [sbuf(st, "sq%d" % i, [128, T], BF16) for i in range(2)]
                sd = [sbuf(st, "sd%d" % i, [128, T], F32) for i in range(2)]
                qn = [sbuf(st, "qn%d" % i, [128, T], BF16, True) for i in range(3)]
                of = [sbuf(st, "of%d" % i, [128, T], F32, True) for i in range(3)]
                vb = [sbuf(st, "vb%d" % i, [128, 512], BF16, True) for i in range(2)]
                def load_x(i):
                    (s, b, g0, nt, q_, mid_) = tiles[i]
                    x_ = xt[i % 2]
                    dma(x_[:], xs_v[g0 // T], writes=[x_.b], ds=x_.ds)

                def stage_a(i):
                    (s_, b_, g0_, nt_, q_, mid_) = tiles[i]
                    ss_, rs_ = ss2[i % 2], rs2[i % 2]
                    P.op("vector", lambda e, ss_=ss_: e.memset(ss_[:], 0.0), reads=[ss_.b], writes=[ss_.b])
                    rms_to_featmajor(xt[i % 2], ss_, rs_, xh2[i % 2], junk, tp, nT2[i % 2], G1, mod, s_)

                load_x(0)
                if len(tiles) > 1:
                    load_x(1)
                stage_a(0)
                cnt = {"qk": 0, "of": 0, "mm": 0, "v": 0}
                pend = []
                for i, (s, b, g0, nt, q_, mid_) in enumerate(tiles):
                    if i + 1 < len(tiles):
                        stage_a(i + 1)
                    if i + 2 < len(tiles):
                        load_x(i + 2)
                    nT = nT2[i % 2]
                    for m in list(range(0, 8)) + list(range(12, 20)):
                        pb = mmp[cnt["mm"] % 4]
                        cnt["mm"] += 1

                        def mmf(e, m=m, pb=pb, nT=nT):
                            ins = None
                            for k in range(8):
                                ins = e.matmul(pb[:], lhsT=winb[:, k, m * 128:(m + 1) * 128], rhs=nT[:, k, :],
                                               start=(k == 0), stop=(k == 7))
                            return ins
                        P.op("tensor", mmf, reads=[nT.b], writes=[pb.b])
                        while pend:
                            pend.pop(0)()
                        if m < 8:
                            j = cnt["qk"]
                            cnt["qk"] += 1
                            sq_, sd_, sp_, qn_ = sq[j % 2], sd[j % 2], ssp[j % 2], qn[j % 3]
                            P.op("scalar", lambda e, pb=pb, sq_=sq_: e.activation(out=sq_[:], in_=pb[:], func=AF.Square),
                                 reads=[pb.b], writes=[sq_.b])
                            def rest(m=m, pb=pb, sq_=sq_, sd_=sd_, sp_=sp_, qn_=qn_, g0=g0):
                                P.op("tensor", lambda e, sq_=sq_, sp_=sp_: e.matmul(sp_[:], lhsT=blk1[:], rhs=sq_[:], start=True, stop=True),
                                     reads=[sq_.b, blk1.b], writes=[sp_.b])
                                P.op("scalar", lambda e, sp_=sp_, sd_=sd_: e.activation(out=sd_[:], in_=sp_[:], func=AF.Ln,
                                                                                     scale=1.0 / 64, bias=epsb[:, 0:1]),
                                     reads=[sp_.b, epsb.b], writes=[sd_.b])
                                P.op("scalar", lambda e, sd_=sd_: e.activation(out=sd_[:], in_=sd_[:], func=AF.Exp, scale=-0.5),
                                     reads=[sd_.b], writes=[sd_.b])
                                g_ = gq8 if m < 4 else gk1
                                P.op("vector", lambda e, pb=pb, sd_=sd_, qn_=qn_, g_=g_: e.scalar_tensor_tensor(
                                    out=qn_[:], in0=pb[:], scalar=g_[:, 0:1], in1=sd_[:], op0=ALU.mult, op1=ALU.mult),
                                    reads=[pb.b, sd_.b, g_.b], writes=[qn_.b])
                                dstv = QT_v if m < 4 else KT_v
                                dma(dstv[:, m % 4, g0:g0 + T], qn_[:], reads=[qn_.b], ds=qn_.ds)
                            pend.append(rest)
                        else:
                            j = cnt["of"]
                            cnt["of"] += 1
                            o_ = of[j % 3]
                            fnc = AF.Copy if m < 16 else AF.Gelu_apprx_tanh
                            P.op("scalar", lambda e, pb=pb, o_=o_, fnc=fnc: e.activation(out=o_[:], in_=pb[:], func=fnc),
                                 reads=[pb.b], writes=[o_.b])
                            dstv = XL_v if m < 16 else GG_v
                            dma(dstv[:, m % 4, g0:g0 + T], o_[:], reads=[o_.b], ds=o_.ds)
                    for sub in range(4):
                        pb = mmp[cnt["mm"] % 4]
                        cnt["mm"] += 1

                        def mmv(e, sub=sub, pb=pb, nT=nT):
                            ins = None
                            for k in range(8):
                                ins = e.matmul(pb[:], lhsT=nT[:, k, sub * 128:(sub + 1) * 128], rhs=winb[:, k, 1024:1536],
                                               start=(k == 0), stop=(k == 7))
                            return ins
                        P.op("tensor", mmv, reads=[nT.b], writes=[pb.b])
                        while pend:
                            pend.pop(0)()
                        v_ = vb[cnt["v"] % 2]
                        cnt["v"] += 1
                        P.op("vector", lambda e, pb=pb, v_=v_: e.tensor_copy(out=v_[:], in_=pb[:]), reads=[pb.b], writes=[v_.b])
                        dma(VV[g0 + sub * 128:g0 + (sub + 1) * 128, :], v_[:], reads=[v_.b], ds=v_.ds)
                P.barrier()
                P.emit()

        def lru_setup(st, dr):
            g = gate_d[dr]
            W = {}
            stg = sbuf(st, "gst_" + dr, [128, 4, 128], F32, True)
            for nm in ("wr", "wi"):
                W[nm] = sbuf(st, nm + "_" + dr, [128, 4, 128], BF16)
                dma(stg[:], g[nm], writes=[stg.b], ds=stg.ds)
                P.op("vector", lambda e, d_=W[nm]: e.tensor_copy(out=d_[:], in_=stg[:]), reads=[stg.b], writes=[W[nm].b])
            for nm in ("br", "bi", "lam"):
                W[nm] = sbuf(st, nm + "_" + dr, [128, 4], F32, True)
                dma(W[nm][:], g[nm], writes=[W[nm].b], ds=W[nm].ds)
            W["kap"] = sbuf(st, "kap_" + dr, [128, 4], F32)
            W["kap2"] = sbuf(st, "kap2_" + dr, [128, 4], F32)
            kap, kap2, lam = W["kap"], W["kap2"], W["lam"]
            P.op("scalar", lambda e: e.activation(out=kap[:], in_=lam[:], func=AF.Exp, scale=-1.0), reads=[lam.b], writes=[kap.b])
            P.op("scalar", lambda e: e.activation(out=kap[:], in_=kap[:], func=AF.Ln, bias=1.0), reads=[kap.b], writes=[kap.b])
            P.op("vector", lambda e: e.tensor_scalar(out=kap2[:], in0=kap[:], scalar1=-16.0, scalar2=None, op0=ALU.mult),
                 reads=[kap.b], writes=[kap2.b])
            P.op("vector", lambda e: e.tensor_scalar(out=kap[:], in0=kap[:], scalar1=-8.0, scalar2=None, op0=ALU.mult),
                 reads=[kap.b], writes=[kap.b])
            return W

        def lru_conv(xl, xc, xcb, cw, cb):
            for c in range(4):
                eng = ("vector", "gpsimd")[c % 2]
                P.op(eng, lambda e, c=c: e.tensor_scalar(out=xc[:, c, :], in0=xl[:, c, 0:T], scalar1=cw[:, c, 0:1],
                                                        scalar2=cb[:, c:c + 1], op0=ALU.mult, op1=ALU.add),
                     reads=[xl.b, cw.b, cb.b, xc.b], writes=[xc.b])
                yield
                for j in range(1, 4):
                    last = (j == 3)
                    P.op("vector", lambda e, c=c, j=j: e.scalar_tensor_tensor(out=xc[:, c, :], in0=xl[:, c, j:j + T],
                                                                        scalar=cw[:, c, j:j + 1], in1=xc[:, c, :],
                                                                        op0=ALU.mult, op1=ALU.add),
                         reads=[xl.b, cw.b, xc.b], writes=[xc.b])
                    yield
                P.op("scalar", lambda e, c=c: e.activation(out=xcb[:, c, :], in_=xc[:, c, :], func=AF.Copy),
                     reads=[xc.b, xcb.b], writes=[xcb.b])
                yield

        def lru_dir(W, xc, xcb, gp, rt, it, at, h, carry, reverse, first):
            for c in range(4):
                pr, pi = gp[(2 * c) % len(gp)], gp[(2 * c + 1) % len(gp)]
                P.op("tensor", lambda e, c=c, pr=pr: e.matmul(pr[:, 0:T], lhsT=W["wr"][:, c, :], rhs=xcb[:, c, :], start=True, stop=True),
                     reads=[xcb.b, W["wr"].b], writes=[pr.b])
                yield
                P.op("scalar", lambda e, c=c, pr=pr: e.activation(out=rt[:, c, :], in_=pr[:, 0:T], func=AF.Sigmoid, bias=W["br"][:, c:c + 1]),
                     reads=[pr.b, W["br"].b, rt.b], writes=[rt.b])
                yield
                P.op("tensor", lambda e, c=c, pi=pi: e.matmul(pi[:, 0:T], lhsT=W["wi"][:, c, :], rhs=xcb[:, c, :], start=True, stop=True),
                     reads=[xcb.b, W["wi"].b], writes=[pi.b])
                yield
                P.op("scalar", lambda e, c=c, pi=pi: e.activation(out=it[:, c, :], in_=pi[:, 0:T], func=AF.Sigmoid, bias=W["bi"][:, c:c + 1]),
                     reads=[pi.b, W["bi"].b, it.b], writes=[it.b])
                yield
            for c in range(4):
                P.op("scalar", lambda e, c=c: e.activation(out=at[:, c, :], in_=rt[:, c, :], func=AF.Exp, scale=W["kap"][:, c:c + 1]),
                     reads=[rt.b, W["kap"].b, at.b], writes=[at.b])
                yield
                P.op("scalar", lambda e, c=c: e.activation(out=rt[:, c, :], in_=rt[:, c, :], func=AF.Exp, scale=W["kap2"][:, c:c + 1]),
                     reads=[rt.b, W["kap2"].b], writes=[rt.b])
                yield
            for c in range(4):
                P.op("scalar", lambda e, c=c: e.activation(out=rt[:, c, :], in_=rt[:, c, :], func=AF.Sqrt, scale=-1.0, bias=1.0),
                     reads=[rt.b], writes=[rt.b])
                yield
                eng = ("vector", "gpsimd")[c % 2]
                P.op(eng, lambda e, c=c: e.tensor_tensor(out=it[:, c, :], in0=it[:, c, :], in1=xc[:, c, :], op=ALU.mult),
                     reads=[it.b, xc.b], writes=[it.b])
                yield
                P.op(eng, lambda e, c=c: e.tensor_tensor(out=it[:, c, :], in0=it[:, c, :], in1=rt[:, c, :], op=ALU.mult),
                     reads=[it.b, rt.b], writes=[it.b])
                yield
            for c in range(4):
                init = 0.0 if first else carry[:, c:c + 1]
                if reverse:
                    fn = lambda e, c=c, init=init: e.tensor_tensor_scan(out=h[:, c, ::-1], data0=at[:, c, ::-1], data1=it[:, c, ::-1],
                                                                        initial=init, op0=ALU.mult, op1=ALU.add)
                else:
                    fn = lambda e, c=c, init=init: e.tensor_tensor_scan(out=h[:, c, :], data0=at[:, c, :], data1=it[:, c, :],
                                                                        initial=init, op0=ALU.mult, op1=ALU.add)
                P.op("vector", fn, reads=[at.b, it.b, h.b, carry.b], writes=[h.b])
                yield
            col = 0 if reverse else T - 1
            P.op("gpsimd", lambda e, col=col: e.tensor_copy(out=carry[:], in_=h[:, :, col]), reads=[h.b, carry.b], writes=[carry.b])
            yield

        def load_xl(xl, s, b, g0, nt, q_=None, mid=None):
            lo = 0 if b == 0 else -1
            hi = T if b == nt - 1 else T + 2
            if b == 0:
                P.op("gpsimd", lambda e: e.memset(xl[:, :, 0:1], 0.0), reads=[xl.b], writes=[xl.b])
            if b == nt - 1:
                P.op("gpsimd", lambda e: e.memset(xl[:, :, T + 1:T + 3], 0.0), reads=[xl.b], writes=[xl.b])
            dma(xl[:, :, 1 + lo:1 + hi], XL_v[:, :, g0 + lo:g0 + hi], reads=[xl.b], writes=[xl.b], ds=xl.ds)
            if mid == "lo":
                P.op("vector", lambda e: e.tensor_scalar(out=xl[:, :, T + 1:T + 3], in0=xl[:, :, T + 1:T + 3], scalar1=flg[:, 0:1],
                                                         scalar2=None, op0=ALU.mult), reads=[xl.b, flg.b], writes=[xl.b])
            if mid == "hi":
                P.op("vector", lambda e: e.tensor_scalar(out=xl[:, :, 0:1], in0=xl[:, :, 0:1], scalar1=flg[:, 0:1],
                                                         scalar2=None, op0=ALU.mult), reads=[xl.b, flg.b], writes=[xl.b])

        def cut_carry(carry):
            P.op("vector", lambda e: e.tensor_scalar(out=carry[:], in0=carry[:], scalar1=flg[:, 0:1], scalar2=None, op0=ALU.mult),
                 reads=[carry.b, flg.b], writes=[carry.b])

        if 2 in phases:
            with ExitStack() as st:
                P.reset_dsem_pool()
                P.next_dsem = 4
                Wb = lru_setup(st, "b")
                cw = sbuf(st, "cw", [128, 4, 4], F32, True)
                cb = sbuf(st, "cb", [128, 4], F32, True)
                dma(cw[:], lcw_d, writes=[cw.b], ds=cw.ds)
                dma(cb[:], lcb_d, writes=[cb.b], ds=cb.ds)
                xl = [sbuf(st, "xl%d" % i, [128, 4, T + 3], F32, True) for i in range(2)]
                xc2 = [sbuf(st, "xc%d" % i, [128, 4, T], F32) for i in range(2)]
                xcb2 = [sbuf(st, "xcb%d" % i, [128, 4, T], BF16) for i in range(2)]
                rt = sbuf(st, "rt", [128, 4, T], F32)
                it = sbuf(st, "it", [128, 4, T], F32)
                at = sbuf(st, "at", [128, 4, T], F32)
                hh = [sbuf(st, "hh%d" % i, [128, 4, T], F32, True) for i in range(2)]
                carry = sbuf(st, "carry", [128, 4], F32)
                gp = [psum(st, "gp%d" % i, [128, 512], F32) for i in range(8)]
                order = []
                for lst in seq_tiles:
                    order += lst[::-1]
                load_xl(xl[0], *order[0])
                if len(order) > 1:
                    load_xl(xl[1], *order[1])
                for _ in lru_conv(xl[0], xc2[0], xcb2[0], cw, cb):
                    pass
                for i, (s, b, g0, nt, q_, mid) in enumerate(order):
                    h = hh[i % 2]
                    g_dir = lru_dir(Wb, xc2[i % 2], xcb2[i % 2], gp, rt, it, at, h, carry, True, b == nt - 1)
                    g_conv = None
                    if i + 1 < len(order):
                        g_conv = lru_conv(xl[(i + 1) % 2], xc2[(i + 1) % 2], xcb2[(i + 1) % 2], cw, cb)
                    alive = True
                    while alive:
                        alive = False
                        for g in (g_dir, g_conv):
                            if g is not None:
                                try:
                                    next(g)
                                    alive = True
                                except StopIteration:
                                    pass
                    if i + 2 < len(order):
                        load_xl(xl[i % 2], *order[i + 2])
                    if mid == "hi":
                        cut_carry(carry)
                    dma(HB_v[:, :, g0:g0 + T], h[:], reads=[h.b], ds=h.ds)
                P.barrier()
                P.emit()

        if 3 in phases:
            with ExitStack() as st:
                P.reset_dsem_pool()
                P.next_dsem = 4
                Wf = lru_setup(st, "f")
                cw = sbuf(st, "cw", [128, 4, 4], F32, True)
                cb = sbuf(st, "cb", [128, 4], F32, True)
                dma(cw[:], lcw_d, writes=[cw.b], ds=cw.ds)
                dma(cb[:], lcb_d, writes=[cb.b], ds=cb.ds)
                woutb = sbuf(st, "woutb", [128, 8, D], BF16)
                EI = sbuf(st, "EI", [128, 8, 11, 64], BF16)
                EF = sbuf(st, "EF", [128, 8, 15, 64], BF16)
                with ExitStack() as st2:
                    wstg = [sbuf(st2, "wstg%d" % i, [128, 8, 512], F32, True) for i in range(2)]
                    load_cast(woutb, w_out, 8, D, wstg)
                    for h in range(8):
                        w = wstg[h % 2]
                        wv_ = w[:].rearrange("p k n -> p (k n)")
                        dma(wv_[:, 0:11 * 64], TI_d[:, h].rearrange("p s c -> p (s c)"), writes=[w.b], ds=w.ds)
                        dma(wv_[:, 1024:1024 + 15 * 64], TF_d[:, h].rearrange("p s c -> p (s c)"), reads=[w.b], writes=[w.b], ds=w.ds)
                        P.op("scalar", lambda e, h=h, wv_=wv_: e.activation(out=EI[:, h].rearrange("p s c -> p (s c)"),
                                                                          in_=wv_[:, 0:11 * 64], func=AF.Exp),
                             reads=[w.b, EI.b], writes=[EI.b])
                        P.op("scalar", lambda e, h=h, wv_=wv_: e.activation(out=EF[:, h].rearrange("p s c -> p (s c)"),
                                                                          in_=wv_[:, 1024:1024 + 15 * 64], func=AF.Exp),
                             reads=[w.b, EF.b], writes=[EF.b])
                    P.barrier()
                    P.emit()
                g1row = sbuf(st, "g1row", [128, D], F32, True)
                xl = [sbuf(st, "xl%d" % i, [128, 4, T + 3], F32, True) for i in range(1)]
                slab = st.enter_context(nc.sbuf_tensor("slab3", [128, 8192], F32))

                def sview(name, ap, dma_=False):
                    tt = Tile.__new__(Tile)
                    tt.t = ap
                    tt.b = P.buf(name)
                    tt.ds = P.new_dsem() if dma_ else None
                    return tt
                xc = sview("xc", slab[:, 0:2048].rearrange("p (c t) -> p c t", c=4))
                rt = sview("rt", slab[:, 2048:4096].rearrange("p (c t) -> p c t", c=4))
                it = sview("it", slab[:, 4096:6144].rearrange("p (c t) -> p c t", c=4))
                at = sview("at", slab[:, 6144:8192].rearrange("p (c t) -> p c t", c=4))
                x1 = sview("x1", slab[:, 0:4096].rearrange("p (c t) -> p c t", c=4), True)
                xh = sview("xh", slab[:, 4096:6144].bitcast(BF16).rearrange("p (c t) -> p c t", c=4))
                n2 = sview("n2", slab[:, 6144:8192].bitcast(BF16).rearrange("p (c t) -> p c t", c=8), True)
                x1.b.al = [xc.b, rt.b]
                xc.b.al = [x1.b]
                rt.b.al = [x1.b]
                xh.b.al = [it.b]
                it.b.al = [xh.b]
                n2.b.al = [at.b]
                at.b.al = [n2.b]
                xcb = sbuf(st, "xcb", [128, 4, T], BF16)
                hh = [sbuf(st, "hh%d" % i, [128, 4, T], F32) for i in range(1)]
                carry = sbuf(st, "carry", [128, 4], F32)
                hbt = sbuf(st, "hbt", [128, 4, T], F32, True)
                ggt = sbuf(st, "ggt", [128, 4, T], F32, True)
                mixT = sbuf(st, "mixT", [128, 4, T], BF16)
                qT = sbuf(st, "qT", [128, 4, T], BF16, True)
                kT = sbuf(st, "kT", [128, 4, 2 * T], BF16, True)
                vt = sbuf(st, "vt", [128, 8, 512], BF16, True)
                ex = [sbuf(st, "ex%d" % i, [128, 2, T], BF16) for i in range(2)]
                pT = [sbuf(st, "pT%d" % i, [128, 8, 2, T], BF16) for i in range(2)]
                rc = sbuf(st, "rc", [128, T], F32)
                xr = [sbuf(st, "xr%d" % i, [128, D], F32, True) for i in range(2)]
                junk = sbuf(st, "junk", [128, D], BF16)
                ss = sbuf(st, "ss", [128, 4], F32)
                rs = sbuf(st, "rs", [128, 4], F32)
                pp = [psum(st, "pp%d" % i, [128, 1024], F32) for i in range(2)]
                nump = psum(st, "nump", [128, 512], F32)
                denp = psum(st, "denp", [128, 1024], F32)
                gp1 = psum(st, "gp1", [128, 512], F32)
                class V_:
                    pass

                def view(tile_, lo, hi, bf=False):
                    v = V_()
                    base = tile_.t
                    v.t = base
                    v.b = tile_.b
                    v.lo = lo
                    v.bf = bf
                    return v
                banks = []

                class Bank:
                    def __init__(self, tile_, lo):
                        self.tile = tile_
                        self.lo = lo
                        self.b = tile_.b

                    def __getitem__(self, idx):
                        a = idx[1].start or 0
                        bb = idx[1].stop if idx[1].stop is not None else 512
                        return self.tile.t[idx[0], self.lo + a:self.lo + bb]
                banks = [Bank(pp[0], 0), Bank(pp[0], 512), Bank(pp[1], 0), Bank(pp[1], 512),
                         Bank(denp, 0), Bank(denp, 512), Bank(nump, 0), Bank(gp1, 0)]
                tp = []
                for tile_ in (nump, gp1):
                    tt = Tile.__new__(Tile)
                    tt.t = tile_.t.bitcast(BF16)
                    tt.b = tile_.b
                    tt.ds = None
                    tp.append(tt)

                attB = sbuf(st, "attB", [128, 4, T], BF16)
                mixL = [sbuf(st, "mixL%d" % i, [128, 4, T], BF16) for i in range(2)]
                gbank = banks

                def lru_loads(i):
                    (s, b, g0, nt, q_, mid) = tiles[i]
                    load_xl(xl[0], s, b, g0, nt, q_, mid)
                    dma(hbt[:], HB_v[:, :, g0:g0 + T], writes=[hbt.b], ds=hbt.ds)
                    dma(ggt[:], GG_v[:, :, g0:g0 + T], writes=[ggt.b], ds=ggt.ds)

                def lru_stream(i):
                    (s, b, g0, nt, q_, mid) = tiles[i]
                    xl_ = xl[0]
                    yield from lru_conv(xl_, xc, xcb, cw, cb)
                    h = hh[0]
                    yield from lru_dir(Wf, xc, xcb, gbank, rt, it, at, h, carry, False, b == 0)
                    if mid == "lo":
                        cut_carry(carry)
                    mL = mixL[i % 2]
                    for c in range(4):
                        P.op("vector", lambda e, c=c, h=h: e.tensor_tensor(out=hbt[:, c, :], in0=hbt[:, c, :], in1=h[:, c, :], op=ALU.add),
                             reads=[hbt.b, h.b], writes=[hbt.b])
                        yield
                        P.op("vector", lambda e, c=c, mL=mL: e.tensor_tensor(out=mL[:, c, :], in0=hbt[:, c, :], in1=ggt[:, c, :], op=ALU.mult),
                             reads=[hbt.b, ggt.b, mL.b], writes=[mL.b])
                        yield

                cur_slot = {"s": -1}
                lru_loads(0)
                for i, (s, b, g0, nt, q_, mid) in enumerate(tiles):
                    if s != cur_slot["s"]:
                        cur_slot["s"] = s
                        dma(g1row[:], bcast_rows(GR[s:s + 1, :]), writes=[g1row.b], ds=g1row.ds)
                    gen = None
                    ld_top, ld_bot = (b == 0), (b == nt - 1)
                    jlo = 0 if ld_top else -2
                    jhi = 3 if ld_bot else 5
                    tok_lo = g0 + jlo * 128
                    tok_hi = g0 + (jhi + 1) * 128
                    dma(qT[:], QT_v[:, :, g0:g0 + T], writes=[qT.b], ds=qT.ds)
                    dma(kT[:, :, (jlo + 2) * 128:(jhi + 3) * 128], KT_v[:, :, tok_lo:tok_hi], writes=[kT.b], ds=kT.ds)
                    dma(vt[:, jlo + 2:jhi + 3, :], VV[tok_lo:tok_hi, :].rearrange("(j p) f -> p j f", p=128),
                        writes=[vt.b], ds=vt.ds)
                    for _ in lru_stream(i):
                        pass
                    if i + 1 < len(tiles):
                        lru_loads(i + 1)

                    def attention(top_t, bot_t, outT, o0, gen=None):
                        jlo = 0 if top_t else -2
                        jhi = 3 if bot_t else 5
                        CH = {}
                        for ii in range(4):
                            ch = [j for j in range(ii - 2, ii + 3) if jlo <= j <= jhi]
                            if top_t and ii == 0:
                                ch = [0, 1, 2, 3]
                            if bot_t and ii == 3:
                                ch = [0, 1, 2, 3]
                            CH[ii] = ch
                        def s_part(hpair):
                            pTt = pT[hpair % 2]
                            for jj in range(jlo, jhi + 1):
                                iis = [ii for ii in range(4) if jj in CH[ii]]
                                q0, q1 = 2 * iis[0], 2 * iis[-1] + 2
                                c0, c1 = q0 * 64, q1 * 64
                                ps = pp[(jj - jlo) % 2]
                                ex_ = ex[(jj - jlo) % 2]

                                def smm(e, jj=jj, ps=ps, c0=c0, c1=c1, hpair=hpair):
                                    ins = None
                                    for e2 in range(2):
                                        ins = e.matmul(ps[:, e2 * 512 + c0:e2 * 512 + c1],
                                                       lhsT=kT[e2 * 64:(e2 + 1) * 64, hpair, (jj + 2) * 128:(jj + 3) * 128],
                                                       rhs=qT[e2 * 64:(e2 + 1) * 64, hpair, c0:c1], start=True, stop=True)
                                    return ins
                                P.op("tensor", smm, reads=[kT.b, qT.b], writes=[ps.b])
                                psv = ps[:].rearrange("p (e c) -> p e c", e=2)
                                P.op("scalar", lambda e, psv=psv, ex_=ex_, c0=c0, c1=c1: e.activation(out=ex_[:, :, c0:c1], in_=psv[:, :, c0:c1], func=AF.Exp),
                                     reads=[ps.b, ex_.b], writes=[ex_.b])
                                segs = []
                                q = q0
                                while q < q1:
                                    rule = "F" if ((top_t and q < 4) or (bot_t and q > 4)) else "I"
                                    qe = q + 1
                                    while qe < q1 and (("F" if ((top_t and qe < 4) or (bot_t and qe > 4)) else "I") == rule):
                                        qe += 1
                                    segs.append((rule, q, qe))
                                    q = qe
                                for (rule, qa, qb) in segs:
                                    if rule == "I":
                                        s0 = 5 - 2 * jj + qa
                                        tab = EI
                                        assert 0 <= s0 and s0 + (qb - qa) <= 11, (s0, qa, qb, jj)
                                    else:
                                        s0 = 7 - 2 * jj + qa
                                        tab = EF
                                        assert 0 <= s0 and s0 + (qb - qa) <= 15, (s0, qa, qb, jj)
                                    nq = qb - qa
                                    P.op("vector", lambda e, tab=tab, s0=s0, nq=nq, qa=qa, qb=qb, ex_=ex_, pTt=pTt, jj=jj, hpair=hpair:
                                         e.tensor_tensor(out=pTt[:, jj + 2, :, qa * 64:qb * 64].rearrange("p e (q c) -> p e q c", c=64),
                                                         in0=ex_[:, :, qa * 64:qb * 64].rearrange("p e (q c) -> p e q c", c=64),
                                                         in1=tab[:, 2 * hpair:2 * hpair + 2, s0:s0 + nq, :], op=ALU.mult),
                                         reads=[ex_.b, tab.b, pTt.b], writes=[pTt.b])
                                if gen is not None:
                                    for _ in range(3):
                                        next(gen, None)
                        def pv_part(hpair):
                            pTt = pT[hpair % 2]
                            def pv(e, hpair=hpair, pTt=pTt, CH=CH):
                                ins = None
                                for ii in range(4):
                                    ch = CH[ii]
                                    for e2 in range(2):
                                        hd = 2 * hpair + e2
                                        for n_, jj in enumerate(ch):
                                            ins = e.matmul(nump[e2 * 64:(e2 + 1) * 64, ii * 128:(ii + 1) * 128],
                                                           lhsT=vt[:, jj + 2, hd * 64:(hd + 1) * 64],
                                                           rhs=pTt[:, jj + 2, e2, ii * 128:(ii + 1) * 128],
                                                           start=(n_ == 0), stop=(n_ == len(ch) - 1))
                                    for e2 in range(2):
                                        for n_, jj in enumerate(ch):
                                            ins = e.matmul(denp[:, e2 * 512 + ii * 128:e2 * 512 + (ii + 1) * 128],
                                                           lhsT=onesb[:], rhs=pTt[:, jj + 2, e2, ii * 128:(ii + 1) * 128],
                                                           start=(n_ == 0), stop=(n_ == len(ch) - 1))
                                return ins
                            P.op("tensor", pv, reads=[vt.b, pTt.b, onesb.b], writes=[nump.b, denp.b])
                            P.op("scalar", lambda e: e.activation(out=rc[0:64, :], in_=denp[0:64, 0:512], func=AF.Ln), reads=[denp.b, rc.b], writes=[rc.b])
                            P.op("scalar", lambda e: e.activation(out=rc[64:128, :], in_=denp[64:128, 512:1024], func=AF.Ln), reads=[denp.b, rc.b], writes=[rc.b])
                            P.op("scalar", lambda e: e.activation(out=rc[:], in_=rc[:], func=AF.Exp, scale=-1.0), reads=[rc.b], writes=[rc.b])
                            P.op("vector", lambda e, hpair=hpair, outT=outT, o0=o0: e.tensor_tensor(out=outT[:, o0 + hpair, :], in0=nump[:], in1=rc[:], op=ALU.mult),
                                 reads=[nump.b, rc.b, outT.b], writes=[outT.b])

                        s_part(0)
                        for hpair_ in range(1, 4):
                            s_part(hpair_)
                            pv_part(hpair_ - 1)
                        pv_part(3)

                    if mid is None:
                        attention(b == 0, b == nt - 1, mixT, 0, gen)
                    else:
                        attention(False, False, mixT, 0, gen)
                        attention(mid == "hi", mid == "lo", attB, 0)
                        P.op("vector", lambda e: e.tensor_scalar(out=mixT[:], in0=mixT[:], scalar1=flg[:, 0:1], scalar2=None, op0=ALU.mult),
                             reads=[mixT.b, flg.b], writes=[mixT.b])
                        P.op("vector", lambda e: e.scalar_tensor_tensor(out=mixT[:], in0=attB[:], scalar=flg[:, 1:2], in1=mixT[:],
                                                                       op0=ALU.mult, op1=ALU.add),
                             reads=[attB.b, flg.b, mixT.b], writes=[mixT.b])
                    if gen is not None:
                        for _ in gen:
                            pass
                    mLc = mixL[i % 2]
                    for sub in range(4):
                        xr_ = xr[sub % 2]
                        dma(xr_[:], xs[g0 + sub * 128:g0 + (sub + 1) * 128, :], writes=[xr_.b], ds=xr_.ds)
                        for half in range(2):
                            bk = banks[(sub * 2 + half) % 4]

                            def omm(e, sub=sub, half=half, bk=bk, mLc=mLc):
                                ins = None
                                for k in range(8):
                                    lt = mixT[:, k, sub * 128:(sub + 1) * 128] if k < 4 else mLc[:, k - 4, sub * 128:(sub + 1) * 128]
                                    ins = e.matmul(bk[:, 0:512], lhsT=lt,
                                                   rhs=woutb[:, k, half * 512:(half + 1) * 512], start=(k == 0), stop=(k == 7))
                                return ins
                            P.op("tensor", omm, reads=[mixT.b, mLc.b], writes=[bk.b])
                            P.op("vector", lambda e, sub=sub, half=half, bk=bk: e.tensor_tensor(
                                out=x1[:, sub, half * 512:(half + 1) * 512], in0=bk[:, 0:512], in1=g1row[:, half * 512:(half + 1) * 512], op=ALU.mult),
                                reads=[bk.b, g1row.b, x1.b], writes=[x1.b])
                            P.op("gpsimd", lambda e, sub=sub, half=half, xr_=xr_: e.tensor_tensor(
                                out=x1[:, sub, half * 512:(half + 1) * 512], in0=x1[:, sub, half * 512:(half + 1) * 512],
                                in1=xr_[:, half * 512:(half + 1) * 512], op=ALU.add),
                                reads=[x1.b, xr_.b], writes=[x1.b])
                    dma(X1_v[g0 // T], x1[:], reads=[x1.b], ds=x1.ds, eng="scalar")
                    P.op("vector", lambda e: e.memset(ss[:], 0.0), reads=[ss.b], writes=[ss.b])
                    rms_to_featmajor(x1, ss, rs, xh, junk, tp, n2, G2, mod, s, sh_off=24)
                    dma(N2_v[:, :, g0:g0 + T], n2[:], reads=[n2.b], ds=n2.ds, eng="scalar")
                P.barrier()
                P.emit()

        if 4 in phases:
            with ExitStack() as st:
                P.reset_dsem_pool()
                P.next_dsem = 4
                wupb = sbuf(st, "wupb", [128, 8, 5632], BF16)
                wdnb = sbuf(st, "wdnb", [128, 22, D], BF16)
                with ExitStack() as st2:
                    wstg = [sbuf(st2, "wstg%d" % i, [128, 8, 512], F32, True) for i in range(2)]
                    load_cast(wupb, w_up, 8, 5632, wstg, piece=512)
                    load_cast(wdnb, w_down, 22, D, wstg, piece=512)
                    P.barrier()
                    P.emit()
                fcw = sbuf(st, "fcw", [128, 44, 3], F32, True)
                fcb = sbuf(st, "fcb", [128, 44], F32, True)
                dma(fcw[:], fcw_d, writes=[fcw.b], ds=fcw.ds)
                dma(fcb[:], fcb_d, writes=[fcb.b], ds=fcb.ds)
                g2row = sbuf(st, "g2row", [128, D], F32, True)
                n2b = [sbuf(st, "n2h%d" % i, [128, 8, T], BF16, True) for i in range(2)]
                ub = [sbuf(st, "ub%d" % i, [128, T + 2], F32) for i in range(4)]
                cg = [sbuf(st, "cg%d" % i, [128, T], F32) for i in range(2)]
                cv = [sbuf(st, "cv%d" % i, [128, T], F32) for i in range(2)]
                sav = sbuf(st, "sav", [128, 44, 2], F32)
                actT = sbuf(st, "actT", [128, 22, T], BF16)
                actT_tail = P.buf("actT_tail")
                MS = 18
                xr = [sbuf(st, "xr%d" % i, [128, D], F32, True) for i in range(2)]
                for x_ in xr:
                    P.op("vector", lambda e, x_=x_: e.memset(x_[:], 0.0), writes=[x_.b])
                upp = [psum(st, "upp%d" % i, [128, 512], F32) for i in range(4)]
                dnp = [psum(st, "dnp%d" % i, [128, 512], F32) for i in range(4)]
                fF = sbuf(st, "fF", [1, D], F32)
                cur_slot = {"s": -1, "q": -1}

                def flush_token(tokL, tag, blend):
                    fl = sbuf(st, "fl" + tag, [128, 44, 1], F32)
                    flb = sbuf(st, "flb" + tag, [128, 22, 1], BF16)
                    fz = sbuf(st, "fz" + tag, [128, 44, 1], F32)
                    P.op("vector", lambda e: e.tensor_tensor(out=fl[:], in0=sav[:, :, 0:1], in1=fcw[:, :, 0:1], op=ALU.mult),
                         reads=[sav.b, fcw.b], writes=[fl.b])
                    P.op("vector", lambda e: e.tensor_tensor(out=fz[:], in0=sav[:, :, 1:2], in1=fcw[:, :, 1:2], op=ALU.mult),
                         reads=[sav.b, fcw.b], writes=[fz.b])
                    P.op("vector", lambda e: e.tensor_tensor(out=fl[:], in0=fl[:], in1=fz[:], op=ALU.add),
                         reads=[fl.b, fz.b], writes=[fl.b])
                    P.op("vector", lambda e: e.tensor_tensor(out=fl[:], in0=fl[:], in1=fcb[:].rearrange("p (m o) -> p m o", o=1), op=ALU.add),
                         reads=[fl.b, fcb.b], writes=[fl.b])
                    P.op("scalar", lambda e: e.activation(out=fl[:, 0:22, :], in_=fl[:, 0:22, :], func=AF.Gelu_apprx_tanh),
                         reads=[fl.b], writes=[fl.b])
                    P.op("vector", lambda e: e.tensor_tensor(out=flb[:], in0=fl[:, 0:22, :], in1=fl[:, 22:44, :], op=ALU.mult),
                         reads=[fl.b], writes=[flb.b])
                    x_ = xr[0]
                    dma(x_[0:1, :], X1[tokL:tokL + 1, :], writes=[x_.b], ds=x_.ds)
                    for half in range(2):
                        pb = dnp[half]

                        def fmm(e, half=half, pb=pb):
                            ins = None
                            for m in range(22):
                                ins = e.matmul(pb[0:1, :], lhsT=flb[:, m, :], rhs=wdnb[:, m, half * 512:(half + 1) * 512],
                                               start=(m == 0), stop=(m == 21))
                            return ins
                        P.op("tensor", fmm, reads=[flb.b], writes=[pb.b])
                        P.op("vector", lambda e, pb=pb, half=half: e.tensor_tensor(out=pb[0:1, :], in0=pb[0:1, :], in1=g2row[0:1, half * 512:(half + 1) * 512], op=ALU.mult),
                             reads=[pb.b, g2row.b], writes=[pb.b])
                        P.op("vector", lambda e, pb=pb, half=half: e.tensor_tensor(
                            out=x_[0:1, half * 512:(half + 1) * 512], in0=pb[0:1, :], in1=x_[0:1, half * 512:(half + 1) * 512], op=ALU.add),
                            reads=[pb.b, x_.b], writes=[x_.b])
                    if blend:
                        P.op("vector", lambda e: e.tensor_scalar(out=fF[:], in0=x_[0:1, :], scalar1=flg[0:1, 1:2], scalar2=None, op0=ALU.mult),
                             reads=[x_.b, flg.b], writes=[fF.b])
                    else:
                        dma(ys[tokL:tokL + 1, :], x_[0:1, :], reads=[x_.b], ds=x_.ds)

                for i, (s, b, g0, nt, q_, mid) in enumerate(tiles):
                    if q_ != cur_slot["q"]:
                        cur_slot["q"] = q_
                        P.op("vector", lambda e: e.memset(sav[:], 0.0), reads=[sav.b], writes=[sav.b])
                    if mid == "hi":
                        flush_token(g0 - 1, "m%d" % q_, True)
                        P.op("vector", lambda e: e.tensor_scalar(out=sav[:], in0=sav[:], scalar1=flg[:, 0:1], scalar2=None, op0=ALU.mult),
                             reads=[sav.b, flg.b], writes=[sav.b])
                    if s != cur_slot["s"]:
                        cur_slot["s"] = s
                        dma(g2row[:], bcast_rows(GR[NS + s:NS + s + 1, :]), writes=[g2row.b], ds=g2row.ds)
                    if i == 0:
                        dma(n2b[0][:], N2_v[:, :, g0:g0 + T], writes=[n2b[0].b], ds=n2b[0].ds)
                    if i + 1 < len(tiles):
                        gn = tiles[i + 1][2]
                        nn = n2b[(i + 1) % 2]
                        dma(nn[:], N2_v[:, :, gn:gn + T], writes=[nn.b], ds=nn.ds)
                    n2 = n2b[i % 2]
                    for m in range(22):
                        for which in range(2):
                            col = which * 2816 + m * 128
                            pb = upp[(2 * m + which) % 4]
                            u_ = ub[(2 * m + which) % 4]
                            ci = which * 22 + m

                            def umm(e, col=col, pb=pb, n2=n2):
                                ins = None
                                for k in range(8):
                                    ins = e.matmul(pb[:], lhsT=wupb[:, k, col:col + 128], rhs=n2[:, k, :], start=(k == 0), stop=(k == 7))
                                return ins
                            P.op("tensor", umm, reads=[n2.b], writes=[pb.b])
                            P.op("scalar", lambda e, u_=u_, pb=pb: e.activation(out=u_[:, 2:T + 2], in_=pb[:], func=AF.Copy),
                                 reads=[pb.b, u_.b], writes=[u_.b])
                            P.op("gpsimd", lambda e, u_=u_, ci=ci: e.tensor_copy(out=u_[:, 0:2], in_=sav[:, ci, :]),
                                 reads=[sav.b, u_.b], writes=[u_.b])
                            P.op("gpsimd", lambda e, u_=u_, ci=ci: e.tensor_copy(out=sav[:, ci, :], in_=u_[:, T:T + 2]),
                                 reads=[u_.b, sav.b], writes=[sav.b])
                            c_ = (cg if which == 0 else cv)[m % 2]
                            eng = "vector"
                            P.op("scalar", lambda e, u_=u_, c_=c_, ci=ci: e.activation(out=c_[:], in_=u_[:, 0:T], func=AF.Identity,
                                                                                      scale=fcw[:, ci, 0:1], bias=fcb[:, ci:ci + 1]),
                                 reads=[u_.b, fcw.b, fcb.b], writes=[c_.b])
                            for j in (1, 2):
                                P.op(eng, lambda e, u_=u_, c_=c_, ci=ci, j=j: e.scalar_tensor_tensor(
                                    out=c_[:], in0=u_[:, j:j + T], scalar=fcw[:, ci, j:j + 1], in1=c_[:], op0=ALU.mult, op1=ALU.add),
                                    reads=[u_.b, fcw.b, c_.b], writes=[c_.b])
                        g_, v_ = cg[m % 2], cv[m % 2]
                        P.op("scalar", lambda e, g_=g_: e.activation(out=g_[:], in_=g_[:], func=AF.Gelu_apprx_tanh), reads=[g_.b], writes=[g_.b])
                        ab_ = actT.b if m < MS else actT_tail
                        P.op("vector", lambda e, g_=g_, v_=v_, m=m: e.tensor_tensor(out=actT[:, m, :], in0=g_[:], in1=v_[:], op=ALU.mult),
                             reads=[g_.b, v_.b, ab_], writes=[ab_])
                    last = (b == nt - 1)
                    for sub in range(2):
                        for half in range(2):
                            pb = dnp[(sub * 2 + half) % 4]

                            def dmm1(e, sub=sub, half=half, pb=pb):
                                ins = None
                                for m in range(MS):
                                    ins = e.matmul(pb[:], lhsT=actT[:, m, sub * 128:(sub + 1) * 128],
                                                   rhs=wdnb[:, m, half * 512:(half + 1) * 512], start=(m == 0), stop=False)
                                return ins
                            P.op("tensor", dmm1, reads=[actT.b], writes=[pb.b])
                    for sub in range(4):
                        tok0 = g0 - 1 + sub * 128
                        p0 = 1 if (b == 0 and sub == 0) else 0
                        x_ = xr[sub % 2]
                        dma(x_[p0:128, :], X1[tok0 + p0:tok0 + 128, :], writes=[x_.b], ds=x_.ds)
                        for half in range(2):
                            pb = dnp[(sub * 2 + half) % 4]

                            m0_ = MS if sub < 2 else 0

                            def dmm(e, sub=sub, half=half, pb=pb, m0_=m0_):
                                ins = None
                                for m in range(m0_, 22):
                                    ins = e.matmul(pb[:], lhsT=actT[:, m, sub * 128:(sub + 1) * 128],
                                                   rhs=wdnb[:, m, half * 512:(half + 1) * 512], start=(m == 0), stop=(m == 21))
                                return ins
                            P.op("tensor", dmm, reads=[actT.b, actT_tail], writes=[pb.b])
                            P.op("vector", lambda e, pb=pb, half=half: e.tensor_tensor(out=pb[:], in0=pb[:], in1=g2row[:, half * 512:(half + 1) * 512], op=ALU.mult),
                                 reads=[pb.b, g2row.b], writes=[pb.b])
                            P.op("vector", lambda e, pb=pb, half=half, x_=x_: e.tensor_tensor(
                                out=x_[:, half * 512:(half + 1) * 512], in0=pb[:], in1=x_[:, half * 512:(half + 1) * 512], op=ALU.add),
                                reads=[pb.b, x_.b], writes=[x_.b])
                        if mid == "hi" and sub == 0:
                            P.op("vector", lambda e, x_=x_: e.scalar_tensor_tensor(out=x_[0:1, :], in0=x_[0:1, :], scalar=flg[0:1, 0:1], in1=fF[:],
                                                                                  op0=ALU.mult, op1=ALU.add),
                                 reads=[x_.b, flg.b, fF.b], writes=[x_.b])
                        dma(ys[tok0 + p0:tok0 + 128, :], x_[p0:128, :], reads=[x_.b], ds=x_.ds)
                    if last:
                        flush_token(g0 + T - 1, "e%d" % q_, False)
                P.barrier()
                P.emit()
    return nc


def _fm(v, nchunk):
    return np.ascontiguousarray(np.asarray(v, np.float32).reshape(nchunk, 128).T)


def _bias_tables(rpb):
    rpb = np.asarray(rpb, np.float32)
    kc = np.arange(64)[:, None]
    c = np.arange(64)[None, :]
    cstart = np.clip(c - 8, 0, 48)
    colok = (kc >= cstart) & (kc < cstart + 16)
    dc = np.clip(kc - c + 15, 0, 30)
    TI = np.full((2, 64, 8, 11, 64), NEG, np.float32)
    TF = np.full((2, 64, 8, 15, 64), NEG, np.float32)
    for a in range(2):
        for s in range(11):
            dr = 12 - s + a
            if 3 <= dr <= 10:
                g = rpb[:, dr][:, dc]
                TI[a, :, :, s, :] = np.where(colok[:, None, :], g.transpose(1, 0, 2), NEG)
        for s in range(15):
            dr = 14 - s + a
            if 0 <= dr <= 14:
                g = rpb[:, dr][:, dc]
                TF[a, :, :, s, :] = np.where(colok[:, None, :], g.transpose(1, 0, 2), NEG)
    return TI.reshape(128, 8, 11, 64), TF.reshape(128, 8, 15, 64)


def _blockdiag(w):
    w = np.asarray(w, np.float32)
    out = np.zeros((128, 4, 128), np.float32)
    for h in range(8):
        c, e = h // 2, h % 2
        out[e * 64:(e + 1) * 64, c, e * 64:(e + 1) * 64] = w[h]
    return out


def make_common(inp):
    TI, TF = _bias_tables(inp["rpb"][0])
    cm = dict(
        w_ada=np.ascontiguousarray(inp["w_ada"][0]), badaT=_fm(inp["b_ada"][0], 48),
        ln1T=_fm(inp["ln1_g"][0], 8), ln2T=_fm(inp["ln2_g"][0], 8),
        w_in=np.ascontiguousarray(inp["w_in"][0]), w_out=np.ascontiguousarray(inp["w_out"][0]),
        w_up=np.ascontiguousarray(inp["w_up"][0]), w_down=np.ascontiguousarray(inp["w_down"][0]),
        gq=np.tile(np.asarray(inp["q_norm_g"][0], np.float32), 2).reshape(128, 1),
        gk=np.tile(np.asarray(inp["k_norm_g"][0], np.float32), 2).reshape(128, 1),
        TI=TI, TF=TF,
        lcw=np.ascontiguousarray(np.asarray(inp["lru_conv_w"][0], np.float32).reshape(4, 4, 128).transpose(2, 1, 0)),
        lcb=_fm(inp["lru_conv_b"][0], 4),
        fcw=np.ascontiguousarray(np.asarray(inp["ffn_conv_w"][0], np.float32).reshape(3, 44, 128).transpose(2, 1, 0)),
        fcb=_fm(inp["ffn_conv_b"][0], 44),
        ident=np.eye(128, dtype=np.float32),
    )
    for dr in ("f", "b"):
        cm["wr_" + dr] = _blockdiag(inp["w_r_" + dr][0])
        cm["wi_" + dr] = _blockdiag(inp["w_i_" + dr][0])
        cm["br_" + dr] = _fm(inp["b_r_" + dr][0], 4)
        cm["bi_" + dr] = _fm(inp["b_i_" + dr][0], 4)
        cm["lam_" + dr] = _fm(inp["lam_" + dr][0], 4)
    return cm


def core_map(cm, xs_list, c_list, flag):
    m = dict(cm)
    m["xs"] = np.ascontiguousarray(np.concatenate([np.asarray(x, np.float32) for x in xs_list], axis=0))
    cs = np.stack([np.asarray(c, np.float32) for c in c_list], axis=0)
    m["cT"] = np.ascontiguousarray(cs.reshape(len(c_list), 8, 128).transpose(2, 1, 0))
    fl = np.empty((128, 2), np.float32)
    fl[:, 0] = float(flag)
    fl[:, 1] = 1.0 - float(flag)
    m["flag"] = fl
    return m


SEQ_SPEC = ((4096, False), (8192, True))
_NC_CACHE = {}


def kernel(**inp):
    xp = np.asarray(inp["x_prompt"], np.float32)
    xsm = np.asarray(inp["x_sample"], np.float32)
    cp = np.asarray(inp["c_prompt"], np.float32)
    csm = np.asarray(inp["c_sample"], np.float32)
    if SEQ_SPEC not in _NC_CACHE:
        _NC_CACHE[SEQ_SPEC] = build_program(SEQ_SPEC)
    nc = _NC_CACHE[SEQ_SPEC]
    cm = make_common(inp)
    zx = np.zeros((4096, D), np.float32)
    zc = np.zeros((D,), np.float32)
    plan = []
    for core in range(4):
        plan.append((3 * core, 3 * core + 1, 3 * core + 2))
    plan.append((12, 13, None))
    plan.append((14, 15, None))
    in_maps = []
    for core in range(8):
        if core < 6:
            ia, ib, ic = plan[core]
            xs_l = [xp[ia], xp[ib], xp[ic] if ic is not None else zx]
            c_l = [cp[ia], cp[ib], cp[ic] if ic is not None else zc]
            in_maps.append(core_map(cm, xs_l, c_l, 0.0))
        else:
            k = core - 6
            in_maps.append(core_map(cm, [zx, xsm[k]], [zc, csm[k], csm[k]], 1.0))
    res = run_bass_kernel_spmd(nc, in_maps, core_ids=list(range(8)))
    yp = np.empty_like(xp)
    ysm = np.empty_like(xsm)
    for core in range(8):
        y = res.results[core]["ys"]
        if core < 6:
            ia, ib, ic = plan[core]
            yp[ia] = y[0:4096]
            yp[ib] = y[4096:8192]
            if ic is not None:
                yp[ic] = y[8192:12288]
        else:
            ysm[core - 6] = y[4096:12288]
    return (yp, ysm)
```
